# Optimizing a Trainium2 kernel written in Bass

```python
import math
import jax, jax.numpy as jnp
from jax import lax
import numpy as np

D_MODEL = 1024
BATCH = 16
SEQ = 256
DEPTH = 2
DEC_BATCH = 4
DEC_SEQ = 1024
PAST_LEN = 256

GRID_W = 64
D_FF = 2816
N_MODS = 9
CONV_DIM = 512
CONV_WIDTH = 31
RET_HEADS = 4
RET_DK = 128
RET_DV = 256
RET_CHUNK = 128
MLA_HEADS = 8
MLA_Q_LORA = 512
MLA_KV_LORA = 256
MLA_D_NOPE = 64
MLA_D_ROPE = 32
MLA_D_V = 64
ROPE_AXIS_HALF = MLA_D_ROPE // 4
ROPE_BASE = 10000.0
Q_BLOCK = 128
N_BRANCHES = 3
DEEPNORM_ALPHA = (2 * DEPTH) ** 0.25
DEEPNORM_BETA = (8 * DEPTH) ** -0.25
LN_EPS = 1e-5
RMS_EPS = 1e-6
MIX_WIDTHS = (CONV_DIM, CONV_DIM, RET_HEADS * RET_DK, RET_HEADS * RET_DK, RET_HEADS * RET_DV,
              RET_HEADS * RET_DV, MLA_Q_LORA, MLA_KV_LORA, MLA_D_ROPE, N_BRANCHES * D_MODEL)
MIX_IN = sum(MIX_WIDTHS)
MIX_SPLITS = tuple(int(s) for s in np.cumsum(MIX_WIDTHS)[:-1])

kernel_name = 'hybrid_flow_prefix_trunk_step'


def layer_norm(x, g, b):
    xf = x.astype(jnp.float32)
    mu = jnp.mean(xf, -1, keepdims=True)
    var = jnp.mean(jnp.square(xf - mu), -1, keepdims=True)
    return ((xf - mu) * lax.rsqrt(var + LN_EPS)).astype(x.dtype) * g + b


def head_norm(x):
    xf = x.astype(jnp.float32)
    mu = jnp.mean(xf, -1, keepdims=True)
    var = jnp.mean(jnp.square(xf - mu), -1, keepdims=True)
    return ((xf - mu) * lax.rsqrt(var + LN_EPS)).astype(x.dtype)


def rms_norm(x, g):
    xf = x.astype(jnp.float32)
    return (xf * lax.rsqrt(jnp.mean(jnp.square(xf), -1, keepdims=True) + RMS_EPS)).astype(x.dtype) * g


def modulate(x, shift, scale):
    return x * (1 + scale) + shift


def swiglu(x, w_in, w_out):
    gate, up = jnp.split(x @ w_in, 2, axis=-1)
    return (jax.nn.silu(gate) * up) @ w_out


def axial_rope_tables(n_tokens, dtype):
    rows = n_tokens // GRID_W
    row_id = jnp.repeat(jnp.arange(rows, dtype=jnp.float32), GRID_W)
    col_id = jnp.tile(jnp.arange(GRID_W, dtype=jnp.float32), rows)
    inv_freq = ROPE_BASE ** (-jnp.arange(ROPE_AXIS_HALF, dtype=jnp.float32) / ROPE_AXIS_HALF)
    ang = jnp.stack([row_id[:, None] * inv_freq, col_id[:, None] * inv_freq], axis=1)
    return jnp.cos(ang).astype(dtype), jnp.sin(ang).astype(dtype)


def apply_axial_rope(x, cos, sin):
    xs = x.reshape(x.shape[:-1] + (2, 2, ROPE_AXIS_HALF))
    x1, x2 = xs[..., 0, :], xs[..., 1, :]
    out = jnp.stack([x1 * cos - x2 * sin, x1 * sin + x2 * cos], axis=-2)
    return out.reshape(x.shape)


def conformer_conv(a, g, w_dw, b_dw, ln_g, ln_b, w_out):
    h = a * jax.nn.sigmoid(g)
    h = lax.conv_general_dilated(h, w_dw[:, None, :], (1,), ((CONV_WIDTH // 2, CONV_WIDTH // 2),),
                                 dimension_numbers=('NWC', 'WIO', 'NWC'),
                                 feature_group_count=CONV_DIM) + b_dw
    h = jax.nn.silu(layer_norm(h, ln_g, ln_b))
    return h @ w_out


def retention_chunkwise(q, k, v, log_g, s0, strict):
    f32 = jnp.float32
    B, T, H, dk = q.shape
    dv = v.shape[-1]
    n = T // RET_CHUNK
    qc = q.astype(f32).reshape(B, n, RET_CHUNK, H, dk)
    kc = k.astype(f32).reshape(B, n, RET_CHUNK, H, dk)
    vc = v.astype(f32).reshape(B, n, RET_CHUNK, H, dv)
    idx = jnp.arange(RET_CHUNK, dtype=f32)
    diff = idx[:, None] - idx[None, :]
    keep = diff > 0 if strict else diff >= 0
    dmask = jnp.where(keep, jnp.exp(jnp.where(keep, diff, 0.0)[None] * log_g[:, None, None]), 0.0)
    scores = jnp.einsum('bnihd,bnjhd->bnhij', qc, kc) * dmask
    inner = jnp.einsum('bnhij,bnjhe->bnihe', scores, vc)
    zeta = jnp.exp((RET_CHUNK - 1.0 - idx)[None, :] * log_g[:, None])
    kv = jnp.einsum('bnjhd,hj,bnjhe->bnhde', kc, zeta, vc)
    chunk_decay = jnp.exp(RET_CHUNK * log_g)[None, :, None, None]

    def step(s, kv_i):
        return chunk_decay * s + kv_i, s

    s_final, s_prev = lax.scan(step, s0.astype(f32), jnp.moveaxis(kv, 1, 0))
    xi = jnp.exp((idx + 1.0)[:, None] * log_g[None, :])
    cross = jnp.einsum('bnihd,nbhde->bnihe', qc, s_prev) * xi[None, None, :, :, None]
    out = (inner + cross).reshape(B, T, H, dv).astype(q.dtype)
    return out, s_final.astype(q.dtype)


def mla_attend(q_nope, q_rope, k_nope, k_rope, v):
    B, Tq, H, _ = q_nope.shape
    nb = Tq // Q_BLOCK
    scale = (MLA_D_NOPE + MLA_D_ROPE) ** -0.5

    def block(qs):
        qn, qr = qs
        s = jnp.einsum('bqhd,bkhd->bhqk', qn, k_nope) + jnp.einsum('bqhd,bkd->bhqk', qr, k_rope)
        p = jax.nn.softmax(s.astype(jnp.float32) * scale, axis=-1).astype(v.dtype)
        return jnp.einsum('bhqk,bkhd->bqhd', p, v)

    def to_blocks(t):
        return jnp.swapaxes(t.reshape(B, nb, Q_BLOCK, H, t.shape[-1]), 0, 1)

    out = lax.map(block, (to_blocks(q_nope), to_blocks(q_rope)))
    return jnp.swapaxes(out, 0, 1).reshape(B, Tq, H, v.shape[-1])


def token_mixer(u, lp, ctx, rope):
    B, T, _ = u.shape
    (glu_a, glu_g, r_q, r_k, r_v, r_g, m_q, m_kv, m_kr, br_g) = jnp.split(u @ lp['mix_w_in'], MIX_SPLITS, axis=-1)
    conv_out = conformer_conv(glu_a, glu_g, lp['conv_w_dw'], lp['conv_b_dw'], lp['conv_ln_g'],
                              lp['conv_ln_b'], lp['conv_w_out'])
    q = r_q.reshape(B, T, RET_HEADS, RET_DK)
    k = r_k.reshape(B, T, RET_HEADS, RET_DK) * (RET_DK ** -0.5)
    v = r_v.reshape(B, T, RET_HEADS, RET_DV)
    log_gf = jax.nn.log_sigmoid(lp['ret_decay_fwd'].astype(jnp.float32))
    log_gb = jax.nn.log_sigmoid(lp['ret_decay_bwd'].astype(jnp.float32))
    if ctx is None:
        s0f = jnp.zeros((B, RET_HEADS, RET_DK, RET_DV), jnp.float32)
        s0b = jnp.zeros((B, RET_HEADS, RET_DK, RET_DV), jnp.float32)
    else:
        s0f, s0b = ctx[2], ctx[3]
    o_f, s_f = retention_chunkwise(q, k, v, log_gf, s0f, False)
    o_b, s_b = retention_chunkwise(q[:, ::-1], k[:, ::-1], v[:, ::-1], log_gb, s0b, True)
    o = head_norm(o_f + o_b[:, ::-1]).reshape(B, T, RET_HEADS * RET_DV)
    ret_out = (jax.nn.silu(r_g) * o) @ lp['ret_w_out']
    q_m = (rms_norm(m_q, lp['mla_q_norm']) @ lp['mla_w_uq']).reshape(B, T, MLA_HEADS, MLA_D_NOPE + MLA_D_ROPE)
    q_nope, q_rope = q_m[..., :MLA_D_NOPE], q_m[..., MLA_D_NOPE:]
    c_kv = rms_norm(m_kv, lp['mla_kv_norm'])
    k_rope = m_kr
    if rope is not None:
        cos, sin = rope
        q_rope = apply_axial_rope(q_rope, cos[:, None], sin[:, None])
        k_rope = apply_axial_rope(k_rope, cos, sin)
    if ctx is None:
        ckv_all, kr_all = c_kv, k_rope
    else:
        ckv_all = jnp.concatenate([ctx[0], c_kv], axis=1)
        kr_all = jnp.concatenate([ctx[1], k_rope], axis=1)
    kv = (ckv_all @ lp['mla_w_ukv']).reshape(B, ckv_all.shape[1], MLA_HEADS, MLA_D_NOPE + MLA_D_V)
    attn = mla_attend(q_nope, q_rope, kv[..., :MLA_D_NOPE], kr_all, kv[..., MLA_D_NOPE:])
    mla_out = attn.reshape(B, T, MLA_HEADS * MLA_D_V) @ lp['mla_w_out']
    g = jax.nn.sigmoid(br_g).reshape(B, T, N_BRANCHES, D_MODEL)
    merged = g[..., 0, :] * conv_out + g[..., 1, :] * ret_out + g[..., 2, :] * mla_out
    out = merged @ lp['mix_w_o']
    new_ctx = (c_kv, k_rope, s_f, s_b) if ctx is None else None
    return out, new_ctx


def trunk_layer(x, cvec, lp, ctx, rope):
    mod = (jax.nn.silu(cvec) @ lp['ada_w'] + lp['ada_b']).reshape(cvec.shape[0], 1, N_MODS, D_MODEL)
    h = modulate(x, mod[:, :, 0], mod[:, :, 1])
    x = layer_norm(DEEPNORM_ALPHA * x + 0.5 * mod[:, :, 2] * swiglu(h, lp['ffn1_w_in'], lp['ffn1_w_out']),
                   lp['post_ln_g'][0], lp['post_ln_b'][0])
    m, new_ctx = token_mixer(modulate(x, mod[:, :, 3], mod[:, :, 4]), lp, ctx, rope)
    x = layer_norm(DEEPNORM_ALPHA * x + mod[:, :, 5] * m, lp['post_ln_g'][1], lp['post_ln_b'][1])
    h = modulate(x, mod[:, :, 6], mod[:, :, 7])
    x = layer_norm(DEEPNORM_ALPHA * x + 0.5 * mod[:, :, 8] * swiglu(h, lp['ffn2_w_in'], lp['ffn2_w_out']),
                   lp['post_ln_g'][2], lp['post_ln_b'][2])
    return x, new_ctx


def setup_inputs(seed: int = 0) -> dict:
    key = jax.random.key(seed)
    ks = iter(jax.random.split(key, 48))
    f32 = jnp.float32

    def nrm(shape, scale):
        return scale * jax.random.normal(next(ks), shape, f32)

    gammas = 1.0 - 2.0 ** (-5.0 - np.arange(RET_HEADS))
    decay_logit = jnp.asarray(np.log(gammas / (1.0 - gammas)), f32)
    gate_offset = jnp.tile(jnp.concatenate([jnp.zeros((2 * D_MODEL,), f32), jnp.ones((D_MODEL,), f32)]), 3)
    beta = DEEPNORM_BETA
    return {
        'x_prompt': nrm((BATCH, SEQ, D_MODEL), 1.0),
        'x_sample': nrm((DEC_BATCH, DEC_SEQ, D_MODEL), 1.0),
        'cache_mla_ckv': nrm((DEC_BATCH, DEPTH, PAST_LEN, MLA_KV_LORA), 1.0),
        'cache_mla_krope': nrm((DEC_BATCH, DEPTH, PAST_LEN, MLA_D_ROPE), 1.0),
        'state_ret_fwd': nrm((DEC_BATCH, DEPTH, RET_HEADS, RET_DK, RET_DV), 0.5),
        'state_ret_bwd': nrm((DEC_BATCH, DEPTH, RET_HEADS, RET_DK, RET_DV), 0.5),
        'c': nrm((DEC_BATCH, D_MODEL), 1.0),
        'c_ctx': nrm((D_MODEL,), 1.0),
        'ada_w': nrm((DEPTH, D_MODEL, N_MODS * D_MODEL), 0.5 * D_MODEL ** -0.5),
        'ada_b': gate_offset + nrm((DEPTH, N_MODS * D_MODEL), 0.02),
        'ffn1_w_in': nrm((DEPTH, D_MODEL, 2 * D_FF), D_MODEL ** -0.5),
        'ffn1_w_out': nrm((DEPTH, D_FF, D_MODEL), beta * D_FF ** -0.5),
        'ffn2_w_in': nrm((DEPTH, D_MODEL, 2 * D_FF), D_MODEL ** -0.5),
        'ffn2_w_out': nrm((DEPTH, D_FF, D_MODEL), beta * D_FF ** -0.5),
        'post_ln_g': 1.0 + nrm((DEPTH, 3, D_MODEL), 0.02),
        'post_ln_b': nrm((DEPTH, 3, D_MODEL), 0.02),
        'mix_w_in': nrm((DEPTH, D_MODEL, MIX_IN), D_MODEL ** -0.5),
        'conv_w_dw': nrm((DEPTH, CONV_WIDTH, CONV_DIM), CONV_WIDTH ** -0.5),
        'conv_b_dw': nrm((DEPTH, CONV_DIM), 0.02),
        'conv_ln_g': 1.0 + nrm((DEPTH, CONV_DIM), 0.02),
        'conv_ln_b': nrm((DEPTH, CONV_DIM), 0.02),
        'conv_w_out': nrm((DEPTH, CONV_DIM, D_MODEL), beta * CONV_DIM ** -0.5),
        'ret_decay_fwd': decay_logit[None] + nrm((DEPTH, RET_HEADS), 0.1),
        'ret_decay_bwd': decay_logit[None] + nrm((DEPTH, RET_HEADS), 0.1),
        'ret_w_out': nrm((DEPTH, RET_HEADS * RET_DV, D_MODEL), beta * (RET_HEADS * RET_DV) ** -0.5),
        'mla_q_norm': 1.0 + nrm((DEPTH, MLA_Q_LORA), 0.02),
        'mla_w_uq': nrm((DEPTH, MLA_Q_LORA, MLA_HEADS * (MLA_D_NOPE + MLA_D_ROPE)), MLA_Q_LORA ** -0.5),
        'mla_kv_norm': 1.0 + nrm((DEPTH, MLA_KV_LORA), 0.02),
        'mla_w_ukv': nrm((DEPTH, MLA_KV_LORA, MLA_HEADS * (MLA_D_NOPE + MLA_D_V)), MLA_KV_LORA ** -0.5),
        'mla_w_out': nrm((DEPTH, MLA_HEADS * MLA_D_V, D_MODEL), beta * (MLA_HEADS * MLA_D_V) ** -0.5),
        'mix_w_o': nrm((DEPTH, D_MODEL, D_MODEL), beta * D_MODEL ** -0.5),
    }


def reference(x_prompt, x_sample, cache_mla_ckv, cache_mla_krope, state_ret_fwd, state_ret_bwd, c, c_ctx,
              ada_w, ada_b, ffn1_w_in, ffn1_w_out, ffn2_w_in, ffn2_w_out, post_ln_g, post_ln_b,
              mix_w_in, conv_w_dw, conv_b_dw, conv_ln_g, conv_ln_b, conv_w_out,
              ret_decay_fwd, ret_decay_bwd, ret_w_out,
              mla_q_norm, mla_w_uq, mla_kv_norm, mla_w_ukv, mla_w_out, mix_w_o):
    rope = axial_rope_tables(x_sample.shape[1], x_sample.dtype)
    h_ctx = x_prompt
    h_lat = x_sample
    c_ctx_row = c_ctx[None, :]
    ckv_l, kr_l, sf_l, sb_l = [], [], [], []
    for l in range(DEPTH):
        lp = {
            'ada_w': ada_w[l], 'ada_b': ada_b[l],
            'ffn1_w_in': ffn1_w_in[l], 'ffn1_w_out': ffn1_w_out[l],
            'ffn2_w_in': ffn2_w_in[l], 'ffn2_w_out': ffn2_w_out[l],
            'post_ln_g': post_ln_g[l], 'post_ln_b': post_ln_b[l],
            'mix_w_in': mix_w_in[l], 'conv_w_dw': conv_w_dw[l], 'conv_b_dw': conv_b_dw[l],
            'conv_ln_g': conv_ln_g[l], 'conv_ln_b': conv_ln_b[l], 'conv_w_out': conv_w_out[l],
            'ret_decay_fwd': ret_decay_fwd[l], 'ret_decay_bwd': ret_decay_bwd[l], 'ret_w_out': ret_w_out[l],
            'mla_q_norm': mla_q_norm[l], 'mla_w_uq': mla_w_uq[l], 'mla_kv_norm': mla_kv_norm[l],
            'mla_w_ukv': mla_w_ukv[l], 'mla_w_out': mla_w_out[l], 'mix_w_o': mix_w_o[l],
        }
        h_ctx, (ckv, kr, sf, sb) = trunk_layer(h_ctx, c_ctx_row, lp, None, None)
        ckv_l.append(ckv)
        kr_l.append(kr)
        sf_l.append(sf)
        sb_l.append(sb)
        ctx = (cache_mla_ckv[:, l], cache_mla_krope[:, l], state_ret_fwd[:, l], state_ret_bwd[:, l])
        h_lat, _ = trunk_layer(h_lat, c, lp, ctx, rope)
    return (h_ctx, h_lat, jnp.stack(ckv_l, axis=1), jnp.stack(kr_l, axis=1),
            jnp.stack(sf_l, axis=1), jnp.stack(sb_l, axis=1))
```

```python
import numpy as np
from contextlib import ExitStack
import concourse.bass as bass
import concourse.mybir as mybir
from concourse.bass_utils import run_bass_kernel_spmd

F32 = mybir.dt.float32
BF16 = mybir.dt.bfloat16
AF = mybir.ActivationFunctionType
ALU = mybir.AluOpType
AX = mybir.AxisListType


class T:
    __slots__ = ("name", "w", "r")

    def __init__(self, name=""):
        self.name = name
        self.w = None
        self.r = []


class DSem:
    def __init__(self, sem):
        self.sem = sem
        self.cnt = 0


class Eng:
    def __init__(self, K, name, sem):
        self.K = K
        self.name = name
        self.sem = sem
        self.cnt = 0
        self.ops = []
        self.waited = {}
        self.pending = False

    def _wait(self, tok):
        if tok is None:
            return
        s, v, owner = tok
        key = id(s)
        if self.waited.get(key, 0) >= v:
            return
        self.waited[key] = v
        self.ops.append(lambda e, s=s, v=v: e.wait_ge(s, v))

    def _deps(self, reads, writes, is_dma=False):
        for t in reads:
            if t.w is not None:
                self._wait(t.w)
        strict = is_dma or self.name != "pe"
        for t in writes:
            if t.w is not None and (strict or t.w[2] is not self):
                self._wait(t.w)
            for r in t.r:
                if strict or r[2] is not self:
                    self._wait(r)

    @staticmethod
    def _record(tok, reads, writes):
        for t in reads:
            if not t.r or t.r[-1] is not tok and t.r[-1] != tok:
                t.r.append(tok)
        for t in writes:
            t.w = tok
            t.r = []

    def op(self, fn, reads=(), writes=(), signal=True):
        self._deps(reads, writes)
        if not signal:
            self.nosig = getattr(self, "nosig", 0) + 1
            if self.nosig >= 12:
                signal = True
        if signal:
            self.nosig = 0
            self.cnt += 1
            sem = self.sem
            self.ops.append(lambda e, fn=fn, sem=sem: fn(e).then_inc(sem, 1))
            tok = (self.sem, self.cnt, self)
            self.pending = False
        else:
            self.ops.append(lambda e, fn=fn: fn(e))
            tok = (self.sem, self.cnt + 1, self)
            self.pending = True
        self._record(tok, reads, writes)
        return tok

    def dma(self, pairs, dsem, reads=(), writes=(), **kw):
        if not isinstance(pairs, list):
            pairs = [pairs]
        self._deps(reads, writes, is_dma=True)
        s = dsem.sem
        for (out, in_) in pairs:
            dsem.cnt += 16
            self.ops.append(lambda e, out=out, in_=in_, s=s, kw=kw: e.dma_start(out=out, in_=in_, **kw).then_inc(s, 16))
        tok = (s, dsem.cnt, None)
        self._record(tok, reads, writes)
        return tok


class Kern:
    def __init__(self, nc, stack):
        self.nc = nc
        self.stack = stack
        self.nsem = 0
        self.nname = 0
        mk = lambda n: Eng(self, n, self.new_sem(n))
        self.pe = mk("pe")
        self.dve = mk("dve")
        self.act = mk("act")
        self.pool = mk("pool")
        self.sp = mk("sp")

    def new_sem(self, name):
        self.nsem += 1
        return self.stack.enter_context(self.nc.semaphore(f"s{self.nsem}_{name}"))

    def dsem(self, name="d"):
        return DSem(self.new_sem(name))

    def sbuf(self, name, shape, dt, stack=None):
        self.nname += 1
        return (stack or self.stack).enter_context(self.nc.sbuf_tensor(f"{name}_{self.nname}", list(shape), dt))

    def psum(self, name, shape, dt=F32):
        return self.stack.enter_context(self.nc.psum_tensor(name, list(shape), dt))

    def finish(self, final_toks):
        best = {}
        for tok in final_toks:
            k = id(tok[0])
            if k not in best or best[k][1] < tok[1]:
                best[k] = tok
        for tok in best.values():
            self.sp._wait(tok)
        for e in (self.pe, self.dve, self.act, self.pool, self.sp):
            assert not e.pending, e.name
        block = self.stack.enter_context(self.nc.Block())
        K = self

        @block.tensor
        def _(e):
            for f in K.pe.ops:
                f(e)

        @block.vector
        def _(e):
            for f in K.dve.ops:
                f(e)

        @block.scalar
        def _(e):
            for f in K.act.ops:
                f(e)

        @block.gpsimd
        def _(e):
            for f in K.pool.ops:
                f(e)

        @block.sync
        def _(e):
            for f in K.sp.ops:
                f(e)


class Phase:
    def __init__(self, K):
        self.K = K
        self.stack = ExitStack()
        self.ts = []

    def sbuf(self, name, shape, dt):
        return self.K.sbuf(name, shape, dt, stack=self.stack)

    def T(self, name=""):
        t = T(name)
        self.ts.append(t)
        return t

    def close(self):
        toks = {}
        for t in self.ts:
            for tok in ([t.w] if t.w is not None else []) + t.r:
                k = id(tok[0])
                if k not in toks or toks[k][1] < tok[1]:
                    toks[k] = tok
        K = self.K
        for e in (K.pe, K.dve, K.act, K.sp):
            for tok in toks.values():
                e._wait(tok)
        self.stack.close()


D = 1024
TOK = 1024
NT = 2
TW = 512
DFF = 2816
NFC = 22
DEPTH = 2
ALPHA = (2 * DEPTH) ** 0.25
LN_EPS = 1e-5
RMS_EPS = 1e-6
MIXW = 7968
NS = 3
SLOT = 6144


import os as _osm


def _os_env(k):
    return _osm.environ.get(k)


def tl(t):
    return slice(t * TW, (t + 1) * TW)


def build_program(stop=None):
    nc = bass.Bass("TRN2", target_bir_lowering=False)
    st = ExitStack()
    K = Kern(nc, st)
    pe, dve, act, pool, sp = K.pe, K.dve, K.act, K.pool, K.sp

    def din(name, shape):
        return nc.dram_tensor(name, list(shape), F32, kind="ExternalInput").ap()

    def dout(name, shape):
        return nc.dram_tensor(name, list(shape), F32, kind="ExternalOutput").ap()

    I = {}
    for name, shape in INPUT_SHAPES.items():
        I[name] = din(name, shape)
    O = {}
    for name, shape in OUTPUT_SHAPES.items():
        O[name] = dout(name, shape)

    xs = K.sbuf("xs", [128, 8, TOK], F32)
    xs_t = [[T(f"xs{c}_{t}") for t in range(NT)] for c in range(8)]
    u = K.sbuf("u", [128, 8, TOK], BF16)
    u_t = [[T(f"u{c}_{t}") for t in range(NT)] for c in range(8)]
    ring = [K.sbuf(f"ring{i}", [128, SLOT], BF16) for i in range(NS)]
    ring_t = [T(f"ring{i}") for i in range(NS)]
    ring_d = [K.dsem(f"ring{i}") for i in range(NS)]
    ring_i = [0]
    PS = [K.psum(f"ps{i}", [128, 512]) for i in range(8)]
    PS_t = [T(f"ps{i}") for i in range(8)]
    ps_i = [0]
    NPS = 7

    def nextps():
        i = ps_i[0] % NPS
        ps_i[0] += 1
        return PS[i], PS_t[i]

    ring_reserved = set()

    def wslot():
        while True:
            i = ring_i[0] % NS
            ring_i[0] += 1
            if i not in ring_reserved:
                return ring[i], ring_t[i], ring_d[i]

    def slot_index(slot):
        return [k for k in range(NS) if ring[k] is slot[0]][0]

    def wload(wap, col0, ncols, kc_n, slot=None, off=0):
        if slot is None:
            slot = wslot()
        s, s_t, s_d = slot
        view = s[:, off:off + kc_n * ncols].rearrange("p (a b) -> p a b", b=ncols)
        src = wap.rearrange("(a p) n -> p a n", p=128)[:, :, col0:col0 + ncols]
        return slot, view, (view, src)

    MARKS = []
    nmm = [0]

    def mark(name):
        MARKS.append((name, nmm[0]))

    def mm(out, lhsT, rhs, start, stop, reads, out_t):
        nmm[0] += 1
        pe.op(lambda e: e.matmul(out, lhsT, rhs, start=start, stop=stop), reads=reads, writes=[out_t], signal=stop)

    cst_d = K.dsem("cst")

    const_ts = []

    def load_const(name, shape, src, dt=F32, eng=None):
        t = K.sbuf(name, shape, dt)
        tt = T(name)
        (eng or sp).dma((t[:], src), cst_d, writes=[tt])
        const_ts.append(tt)
        return t, tt

    def seal_consts(ts, dsem):
        for tt in ts:
            tt.w = (dsem.sem, dsem.cnt, None)

    onesD = K.sbuf("onesD", [128, 128], BF16)
    onesD_t = T("onesD")
    dve.op(lambda e: e.memset(onesD[:], 1.0 / 1024.0), writes=[onesD_t])
    epsln = K.sbuf("epsln", [128, 1], F32)
    epsln_t = T("epsln")
    dve.op(lambda e: e.memset(epsln[:], LN_EPS), writes=[epsln_t])

    cond, cond_t = load_const("cond", [128, 8], I["cond"])
    adab, adab_t = load_const("adab", [128, DEPTH, 72], I["ada_bL"].rearrange("l p j -> p l j"))
    lng, lng_t = load_const("lng", [128, DEPTH, 24], I["lngL"].rearrange("l p j -> p l j"))
    lnb, lnb_t = load_const("lnb", [128, DEPTH, 24], I["lnbL"].rearrange("l p j -> p l j"))

    identf, identf_t = load_const("identf", [128, 128], I["ident"])
    cM, cM_t = load_const("cM", [128, 6, 128], I["retc"].rearrange("k p i -> p k i"))
    colz, colz_t = load_const("colz", [128, 2], I["colz"])
    flags, flags_t = load_const("flags", [128, 4], I["flags"])
    ropeC, ropeC_t = load_const("ropeC", [96, TOK], I["ropeC"])
    ropeS, ropeS_t = load_const("ropeS", [96, TOK], I["ropeS"])
    wdw, wdw_t = load_const("wdw", [128, DEPTH, 4 * 31], I["conv_wdwL"].rearrange("l p c j -> p l (c j)"))
    cvec, cvec_t = load_const("cvec", [128, DEPTH, 12], I["conv_vecL"].rearrange("l p j -> p l j"))
    rdec, rdec_t = load_const("rdec", [128, DEPTH, 8], I["ret_decayL"].rearrange("l p j -> p l j"))
    mlan, mlan_t = load_const("mlan", [128, DEPTH, 6], I["mla_normL"].rearrange("l p j -> p l j"))
    seal_consts(const_ts, cst_d)

    xs_d = K.dsem("xs")
    xT = I["xT"].rearrange("(c p) t -> p c t", p=128)
    for c in range(8):
        sp.dma((xs[:, c, :], xT[:, c, :]), xs_d, writes=[xs_t[c][0], xs_t[c][1]])
    seal_consts([xs_t[c][t] for c in range(8) for t in range(NT)], xs_d)

    csil = K.sbuf("csil", [128, 8], BF16)
    csil_t = T("csil")
    act.op(lambda e: e.activation(out=csil[:], in_=cond[:], func=AF.Silu), reads=[cond_t], writes=[csil_t])
    identb = K.sbuf("identb", [128, 128], BF16)
    identb_t = T("identb")
    act.op(lambda e: e.activation(out=identb[:], in_=identf[:], func=AF.Copy), reads=[identf_t], writes=[identb_t])

    mod = K.sbuf("mod", [128, DEPTH, 72], F32)
    mod_t = [[T(f"mod{l}_{j}") for j in range(9)] for l in range(DEPTH)]
    tabP = K.sbuf("tabP", [128, DEPTH, 24], F32)
    tabQ = K.sbuf("tabQ", [128, DEPTH, 24], F32)
    tabG = K.sbuf("tabG", [128, DEPTH, 24], F32)
    tabAG = K.sbuf("tabAG", [128, DEPTH, 24], F32)
    tabAB = K.sbuf("tabAB", [128, DEPTH, 24], F32)
    tab_t = [[T(f"tab{l}_{s}") for s in range(3)] for l in range(DEPTH)]
    tabA_t = T("tabA")
    dve.op(lambda e: e.tensor_scalar(out=tabAG[:], in0=lng[:], scalar1=ALPHA, scalar2=None, op0=ALU.mult), reads=[lng_t], writes=[tabA_t])
    dve.op(lambda e: e.tensor_scalar(out=tabAB[:], in0=lnb[:], scalar1=ALPHA, scalar2=None, op0=ALU.mult), reads=[lnb_t], writes=[tabA_t])

    MODPS = PS[7]
    MODPS_t = PS_t[7]

    ada_done = {}

    def ada_half(l, j, half, defer=False):
        slot, view, pair = wload(I["ada_w"][l], j * 1024 + half * 512, 512, 8)
        pool.dma([pair], slot[2], writes=[slot[1]])
        for cc in range(4):
            c = half * 4 + cc
            col = (l * 72 + j * 8 + c)
            for kc in range(8):
                mm(MODPS[:, col:col + 1], view[:, kc, cc * 128:(cc + 1) * 128], csil[:, kc:kc + 1], kc == 0, kc == 7,
                   [slot[1], csil_t], MODPS_t)
        ada_done[(l, j)] = ada_done.get((l, j), 0) + 1
        if ada_done[(l, j)] == 2:
            def fin():
                c0 = l * 72 + j * 8
                dve.op(lambda e: e.tensor_tensor(out=mod[:, l, j * 8:(j + 1) * 8], in0=MODPS[:, c0:c0 + 8], in1=adab[:, l, j * 8:(j + 1) * 8], op=ALU.add),
                       reads=[MODPS_t, adab_t], writes=[mod_t[l][j]])
                if j % 3 == 2:
                    tables(l, j // 3)
            if defer:
                return fin
            fin()
        return None

    def ada(l, j):
        ada_half(l, j, 0)
        ada_half(l, j, 1)

    def ada_jobs(l, s):
        return [(l, j, h) for j in (3 * s, 3 * s + 1, 3 * s + 2) for h in range(2)]

    ADAQ = [job for l_ in range(DEPTH) for s_ in range(3) if (l_, s_) != (0, 0) for job in ada_jobs(l_, s_)]
    ADA_FINS = []

    def host(n):
        for _ in range(n):
            if not ADAQ:
                return
            f_ = ada_half(*ADAQ.pop(0), defer=True)
            if f_ is not None:
                ADA_FINS.append(f_)

    def flush_fins():
        while ADA_FINS:
            ADA_FINS.pop(0)()

    def need(l, s):
        while any((jl, jj // 3) == (l, s) for (jl, jj, jh) in ADAQ):
            host(1)
        flush_fins()

    def tables(l, s):
        sh = mod[:, l, (3 * s) * 8:(3 * s + 1) * 8]
        sc = mod[:, l, (3 * s + 1) * 8:(3 * s + 2) * 8]
        gt = mod[:, l, (3 * s + 2) * 8:(3 * s + 3) * 8]
        P = tabP[:, l, s * 8:(s + 1) * 8]
        Q = tabQ[:, l, s * 8:(s + 1) * 8]
        G = tabG[:, l, s * 8:(s + 1) * 8]
        rd = [mod_t[l][3 * s], mod_t[l][3 * s + 1], mod_t[l][3 * s + 2], lng_t, lnb_t]
        wr = [tab_t[l][s]]
        first = (l == 0 and s == 0)
        dve.op(lambda e: e.tensor_scalar(out=P, in0=sc, scalar1=1.0, scalar2=None, op0=ALU.add), reads=rd, writes=wr)
        if first:
            dve.op(lambda e: e.tensor_copy(out=Q, in_=sh), reads=rd, writes=wr)
        else:
            pl, ps_ = (l, s - 1) if s > 0 else (l - 1, 2)
            gp = lng[:, pl, ps_ * 8:(ps_ + 1) * 8]
            bp = lnb[:, pl, ps_ * 8:(ps_ + 1) * 8]
            dve.op(lambda e: e.tensor_tensor(out=Q, in0=bp, in1=P, op=ALU.mult), reads=rd + wr, writes=wr)
            dve.op(lambda e: e.tensor_tensor(out=Q, in0=Q, in1=sh, op=ALU.add), reads=rd + wr, writes=wr)
            dve.op(lambda e: e.tensor_tensor(out=P, in0=gp, in1=P, op=ALU.mult), reads=rd + wr, writes=wr)
        gs = 1.0 if s == 1 else 0.5
        dve.op(lambda e: e.tensor_scalar(out=G, in0=gt, scalar1=gs, scalar2=None, op0=ALU.mult), reads=rd, writes=wr)

    def ffn(l, w_in, w_out, s, jobs=(), hooks=(), post=None, nhost_tail=0):
        jobs = list(jobs)
        hooks = list(hooks)
        mark(f"ffn{l}_{s}")
        ph = Phase(K)
        hid = ph.sbuf("hid", [128, NFC, TOK], BF16)
        hid_t = [[ph.T() for t in range(NT)] for fc in range(NFC)]
        sgb = [ph.sbuf("sg", [128, TW], F32) for _ in range(3)]
        sgb_t = [ph.T() for _ in range(3)]
        if _os_env("KMARKS"):
            print("SBUF remaining in FFN:", nc.sbuf_bytes_remaining)
        k = 0
        for blk in range(11):
            slot = wslot()
            _, vg, pg = wload(w_in, blk * 256, 256, 8, slot=slot, off=0)
            _, vu, pu = wload(w_in, DFF + blk * 256, 256, 8, slot=slot, off=2048)
            pool.dma([pg, pu], slot[2], writes=[slot[1]])
            for j in range(2):
                fc = blk * 2 + j
                for t in range(NT):
                    g_ps, g_t = nextps()
                    u_ps, up_t = nextps()
                    for kc in range(8):
                        mm(g_ps[:], vg[:, kc, j * 128:(j + 1) * 128], u[:, kc, tl(t)], kc == 0, kc == 7, [slot[1], u_t[kc][t]], g_t)
                    for kc in range(8):
                        mm(u_ps[:], vu[:, kc, j * 128:(j + 1) * 128], u[:, kc, tl(t)], kc == 0, kc == 7, [slot[1], u_t[kc][t]], up_t)
                    sg, sg_t = sgb[k % 3], sgb_t[k % 3]
                    k += 1
                    act.op(lambda e, sg=sg, g_ps=g_ps: e.activation(out=sg[:], in_=g_ps[:], func=AF.Silu), reads=[g_t], writes=[sg_t])
                    dve.op(lambda e, sg=sg, u_ps=u_ps, fc=fc, t=t: e.tensor_tensor(out=hid[:, fc, tl(t)], in0=u_ps[:], in1=sg[:], op=ALU.mult),
                           reads=[up_t, sg_t], writes=[hid_t[fc][t]])
            if blk % 2 == 1:
                host(1)
            if hooks and blk % 2 == 0 and blk >= 2:
                hooks.pop(0)()
        if nhost_tail:
            host(nhost_tail)
        flush_fins()
        while hooks:
            hooks.pop(0)()
        mark(f"ffn{l}_{s}_out")
        for blk in range(4):
            slot, view, pair = wload(w_out, blk * 256, 256, NFC)
            pool.dma([pair], slot[2], writes=[slot[1]])
            for j in range(2):
                c = blk * 2 + j
                for t in range(NT):
                    o_ps, o_t = nextps()
                    for kc in range(NFC):
                        mm(o_ps[:], view[:, kc, j * 128:(j + 1) * 128], hid[:, kc, tl(t)], kc == 0, kc == NFC - 1, [slot[1], hid_t[kc][t]], o_t)
                    dve.op(lambda e, o_ps=o_ps, c=c, t=t: e.scalar_tensor_tensor(out=xs[:, c, tl(t)], in0=o_ps[:], scalar=tabG[:, l, s * 8 + c:s * 8 + c + 1],
                                                                                 in1=xs[:, c, tl(t)], op0=ALU.mult, op1=ALU.add),
                           reads=[o_t, tab_t[l][s], xs_t[c][t]], writes=[xs_t[c][t]])
        if post is not None:
            post()
        ph.close()

    def layer_norm(l, s, final=False, jobs=()):
        nl, ns_ = (l, s + 1) if s < 2 else (l + 1, 0)
        mark(f"ln{l}_{s}")
        ph = Phase(K)
        ybf = [ph.sbuf("ybf", [128, 8, TW], BF16) for _ in range(NT)]
        ysq = [ph.sbuf("ysq", [128, 8, TW], BF16) for _ in range(NT)]
        ybf_t = [ph.T() for _ in range(NT)]
        ysq_t = [ph.T() for _ in range(NT)]
        tmp = [ph.sbuf("lntmp", [128, TW], F32) for _ in range(NT)]
        tmp_t = [ph.T() for _ in range(NT)]
        rsd = [ph.sbuf("lnrsd", [128, TW], F32) for _ in range(NT)]
        rsd_t = [ph.T() for _ in range(NT)]
        if _os_env("KMARKS"):
            print("SBUF remaining in LN:", nc.sbuf_bytes_remaining)
        gtab = lng if final else tabAG
        btab = lnb if final else tabAB
        gb_t = [lng_t, lnb_t] if final else [tabA_t]
        if not final:
            need(nl, ns_)
        for t in range(NT):
            for c in range(8):
                act.op(lambda e, c=c, t=t: e.activation(out=ybf[t][:, c, :], in_=xs[:, c, tl(t)], func=AF.Copy), reads=[xs_t[c][t]], writes=[ybf_t[t]])
                act.op(lambda e, c=c, t=t: e.activation(out=ysq[t][:, c, :], in_=xs[:, c, tl(t)], func=AF.Square), reads=[xs_t[c][t]], writes=[ysq_t[t]])
        stat = []
        for t in range(NT):
            m_ps, m_t = nextps()
            e_ps, e_t = nextps()
            for c in range(8):
                mm(m_ps[:], onesD[:], ybf[t][:, c, :], c == 0, c == 7, [onesD_t, ybf_t[t]], m_t)
            for c in range(8):
                mm(e_ps[:], onesD[:], ysq[t][:, c, :], c == 0, c == 7, [onesD_t, ysq_t[t]], e_t)
            stat.append((m_ps, m_t, e_ps, e_t))
        host(6 if s == 0 else 4)
        nr = []
        for t in range(NT):
            m_ps, m_t, e_ps, e_t = stat[t]
            act.op(lambda e, t=t, m_ps=m_ps: e.activation(out=tmp[t][:], in_=m_ps[:], func=AF.Square), reads=[m_t], writes=[tmp_t[t]])
            dve.op(lambda e, t=t, e_ps=e_ps: e.tensor_tensor(out=tmp[t][:], in0=e_ps[:], in1=tmp[t][:], op=ALU.subtract), reads=[e_t, tmp_t[t]], writes=[tmp_t[t]])
            act.op(lambda e, t=t: e.activation(out=tmp[t][:], in_=tmp[t][:], func=AF.Ln, bias=epsln[:], scale=1.0), reads=[tmp_t[t], epsln_t], writes=[tmp_t[t]])
            act.op(lambda e, t=t: e.activation(out=rsd[t][:], in_=tmp[t][:], func=AF.Exp, scale=-0.5), reads=[tmp_t[t]], writes=[rsd_t[t]])
            nm_ps, nm_t = nextps()
            r_ps, r_t = nextps()
            dve.op(lambda e, t=t, nm_ps=nm_ps, m_ps=m_ps: e.scalar_tensor_tensor(out=nm_ps[:], in0=m_ps[:], scalar=-1.0, in1=rsd[t][:], op0=ALU.mult, op1=ALU.mult),
                   reads=[m_t, rsd_t[t]], writes=[nm_t])
            act.op(lambda e, t=t, r_ps=r_ps: e.activation(out=r_ps[:], in_=rsd[t][:], func=AF.Copy), reads=[rsd_t[t]], writes=[r_t])
            nr.append((nm_ps, nm_t, r_ps, r_t))
        for t in range(NT):
            nm_ps, nm_t, r_ps, r_t = nr[t]
            for c in range(8):
                dve.op(lambda e, c=c, t=t, r_ps=r_ps: e.tensor_tensor(out=xs[:, c, tl(t)], in0=xs[:, c, tl(t)], in1=r_ps[:], op=ALU.mult),
                       reads=[xs_t[c][t], r_t], writes=[xs_t[c][t]])
                dve.op(lambda e, c=c, t=t, nm_ps=nm_ps: e.tensor_tensor(out=xs[:, c, tl(t)], in0=xs[:, c, tl(t)], in1=nm_ps[:], op=ALU.add),
                       reads=[xs_t[c][t], nm_t], writes=[xs_t[c][t]])
                if not final:
                    act.op(lambda e, c=c, t=t: e.activation(out=u[:, c, tl(t)], in_=xs[:, c, tl(t)], func=AF.Identity,
                                                            bias=tabQ[:, nl, ns_ * 8 + c:ns_ * 8 + c + 1], scale=tabP[:, nl, ns_ * 8 + c:ns_ * 8 + c + 1]),
                           reads=[xs_t[c][t], tab_t[nl][ns_]], writes=[u_t[c][t]])
            for c in range(8):
                if c < 4:
                    dve.op(lambda e, c=c, t=t: e.tensor_scalar(out=xs[:, c, tl(t)], in0=xs[:, c, tl(t)], scalar1=gtab[:, l, s * 8 + c:s * 8 + c + 1],
                                                               scalar2=btab[:, l, s * 8 + c:s * 8 + c + 1], op0=ALU.mult, op1=ALU.add),
                           reads=[xs_t[c][t]] + gb_t, writes=[xs_t[c][t]])
                else:
                    act.op(lambda e, c=c, t=t: e.activation(out=xs[:, c, tl(t)], in_=xs[:, c, tl(t)], func=AF.Identity,
                                                            bias=btab[:, l, s * 8 + c:s * 8 + c + 1], scale=gtab[:, l, s * 8 + c:s * 8 + c + 1]),
                           reads=[xs_t[c][t]] + gb_t, writes=[xs_t[c][t]])
        flush_fins()
        ph.close()


    onesC = K.sbuf("onesC", [128, 128], BF16)
    onesC_t = T("onesC")
    dve.op(lambda e: e.memset(onesC[:], 1.0 / 512.0), writes=[onesC_t])
    onesK = K.sbuf("onesK", [128, 128], BF16)
    onesK_t = T("onesK")
    dve.op(lambda e: e.memset(onesK[:], 1.0 / 256.0), writes=[onesK_t])
    ones64 = K.sbuf("ones64", [128, 64], BF16)
    ones64_t = T("ones64")
    dve.op(lambda e: e.memset(ones64[:], 1.0), writes=[ones64_t])
    epsrms = K.sbuf("epsrms", [128, 1], F32)
    epsrms_t = T("epsrms")
    dve.op(lambda e: e.memset(epsrms[:], RMS_EPS), writes=[epsrms_t])
    one1 = K.sbuf("one1", [128, 1], F32)
    one1_t = T("one1")
    dve.op(lambda e: e.memset(one1[:], 1.0), writes=[one1_t])


    QTa = [K.sbuf(f"QTa{b}", [128, TOK], BF16) for b in range(2)]
    QTa_t = [T(f"QTa{b}") for b in range(2)]
    KTa = [K.sbuf(f"KTa{b}", [128, 1280], BF16) for b in range(2)]
    KTa_t = [T(f"KTa{b}") for b in range(2)]
    msk_d = K.dsem("msk")
    for b in range(2):
        pool.dma((QTa[b][96:128, :], I["qmask"]), msk_d, writes=[QTa_t[b]])
        pool.dma((KTa[b][96:128, :], I["kmask"]), msk_d, writes=[KTa_t[b]])
    seal_consts(QTa_t + KTa_t, msk_d)

    class Bundle:
        pass

    def mixer_bufs(ph):
        B = Bundle()
        B.hc = ph.sbuf("hc", [128, 4, TOK], BF16)
        B.hc_t = [[ph.T() for t in range(NT)] for c in range(4)]
        B.ogT = ph.sbuf("ogT", [128, 8, TOK], BF16)
        B.ogT_t = [[ph.T() for t in range(NT)] for c in range(8)]
        B.attnT = ph.sbuf("attnT", [128, 4, TOK], BF16)
        B.attnT_t = [[ph.T() for t in range(NT)] for c in range(4)]
        return B

    Dcomb = K.sbuf("Dcomb", [128, 4, 128], F32)
    XIf = K.sbuf("XIf", [128, 4, 128], F32)
    XIb = K.sbuf("XIb", [128, 4, 128], F32)
    rsm = K.sbuf("rsm", [128, 40], F32)
    rtab_t = T("rtab")
    sinit_d = [K.dsem("sinit") for d in range(2)]
    sseg_d = [[K.dsem("sseg") for s_ in range(4)] for d in range(2)]
    ckvo_d = K.dsem("ckvo")
    kro_d = K.dsem("kro")
    cch_d = K.dsem("cch")
    rd_d = [K.dsem("rd") for _ in range(2)]
    out_toks = []

    def bc_mid(ap2d, n):
        a = ap2d.ap
        return bass.AP(ap2d.tensor, ap2d.offset, [list(a[0]), [0, n], list(a[-1])])

    def bc_last(ap2d, n):
        a = ap2d.ap
        return bass.AP(ap2d.tensor, ap2d.offset, [list(a[0]), list(a[-1]), [0, n]])

    def ret_table_parts(l):
        lg = rsm[:, 0:8]
        tmp8 = rsm[:, 32:40]
        rd = [rdec_t, one1_t, cM_t, colz_t, flags_t, rtab_t]
        wr = [rtab_t]
        kscale = 128.0 ** -0.5

        def pre():
            act.op(lambda e: e.activation(out=tmp8, in_=rdec[:, l, :], func=AF.Exp, scale=-1.0), reads=rd, writes=wr)
            act.op(lambda e: e.activation(out=lg, in_=tmp8, func=AF.Ln, bias=one1[:], scale=1.0), reads=rd, writes=wr)
            dve.op(lambda e: e.tensor_scalar(out=lg, in0=lg, scalar1=-1.0, scalar2=None, op0=ALU.mult), reads=rd, writes=wr)

        def head(h):
            lf = rsm[:, h:h + 1]
            lb = rsm[:, 4 + h:5 + h]
            act.op(lambda e: e.activation(out=Dcomb[:, h, :], in_=cM[:, 0, :], func=AF.Exp, scale=lf), reads=rd, writes=wr)
            dve.op(lambda e: e.tensor_tensor(out=Dcomb[:, h, :], in0=Dcomb[:, h, :], in1=cM[:, 2, :], op=ALU.mult), reads=rd, writes=wr)
            act.op(lambda e: e.activation(out=XIb[:, h, :], in_=cM[:, 1, :], func=AF.Exp, scale=lb), reads=rd, writes=wr)
            dve.op(lambda e: e.tensor_tensor(out=XIb[:, h, :], in0=XIb[:, h, :], in1=cM[:, 3, :], op=ALU.mult), reads=rd, writes=wr)
            dve.op(lambda e: e.tensor_tensor(out=Dcomb[:, h, :], in0=Dcomb[:, h, :], in1=XIb[:, h, :], op=ALU.add), reads=rd, writes=wr)
            act.op(lambda e: e.activation(out=XIf[:, h, :], in_=cM[:, 4, :], func=AF.Exp, scale=lf), reads=rd, writes=wr)
            act.op(lambda e: e.activation(out=XIb[:, h, :], in_=cM[:, 5, :], func=AF.Exp, scale=lb), reads=rd, writes=wr)
            act.op(lambda e: e.activation(out=rsm[:, 8 + h:9 + h], in_=colz[:, 0:1], func=AF.Exp, scale=lf), reads=rd, writes=wr)
            act.op(lambda e: e.activation(out=rsm[:, 12 + h:13 + h], in_=colz[:, 1:2], func=AF.Exp, scale=lb), reads=rd, writes=wr)

        def post():
            dve.op(lambda e: e.tensor_scalar(out=rsm[:, 8:16], in0=rsm[:, 8:16], scalar1=kscale, scalar2=None, op0=ALU.mult), reads=rd, writes=wr)
            act.op(lambda e: e.activation(out=rsm[:, 16:24], in_=lg, func=AF.Exp, scale=128.0), reads=rd, writes=wr)
            dve.op(lambda e: e.tensor_scalar(out=rsm[:, 24:32], in0=rsm[:, 16:24], scalar1=flags[:, 0:1], scalar2=None, op0=ALU.mult), reads=rd, writes=wr)

        return [pre] + [(lambda h=h: head(h)) for h in range(4)] + [post]

    def ret_tables(l):
        for f_ in ret_table_parts(l):
            f_()

    def norm_stats(ph, ps_list, ones_t_ap, ones_tt, sq_list, sq_tt, eps_ap, eps_tt, want_mean):
        raise NotImplementedError

    def conv_branch(l, B):
        mark(f"conv{l}")
        ph = Phase(K)
        hpad = ph.sbuf("hpad", [128, 4, 4, 286], BF16)
        hpad_t = [ph.T() for c in range(4)]
        for c in range(4):
            dve.op(lambda e, c=c: e.memset(hpad[:, c, :, :], 0.0), writes=[hpad_t[c]])
        sgb = [ph.sbuf("csg", [128, TW], F32) for _ in range(2)]
        sgb_t = [ph.T() for _ in range(2)]
        k = 0
        for c in range(4):
            if c % 2 == 0:
                slot = wslot()
                _, va, pa = wload(I["mix_w_in"][l], (c // 2) * 256, 256, 8, slot=slot, off=0)
                _, vgl, pgl = wload(I["mix_w_in"][l], 512 + (c // 2) * 256, 256, 8, slot=slot, off=2048)
                pool.dma([pa, pgl], slot[2], writes=[slot[1]])
            cc = c % 2
            for t in range(NT):
                a_ps, a_t = nextps()
                g_ps, g_t = nextps()
                for kc in range(8):
                    mm(a_ps[:], va[:, kc, cc * 128:(cc + 1) * 128], u[:, kc, tl(t)], kc == 0, kc == 7, [slot[1], u_t[kc][t]], a_t)
                for kc in range(8):
                    mm(g_ps[:], vgl[:, kc, cc * 128:(cc + 1) * 128], u[:, kc, tl(t)], kc == 0, kc == 7, [slot[1], u_t[kc][t]], g_t)
                sg, sg_t = sgb[k % 2], sgb_t[k % 2]
                k += 1
                act.op(lambda e, sg=sg, g_ps=g_ps: e.activation(out=sg[:], in_=g_ps[:], func=AF.Sigmoid), reads=[g_t], writes=[sg_t])
                dve.op(lambda e, sg=sg, a_ps=a_ps, c=c, t=t: e.tensor_tensor(out=hpad[:, c, 2 * t:2 * t + 2, 15:271],
                                                                             in0=a_ps[:].rearrange("p (a b) -> p a b", b=256),
                                                                             in1=sg[:].rearrange("p (a b) -> p a b", b=256), op=ALU.mult),
                       reads=[a_t, sg_t], writes=[hpad_t[c]])
            dve.op(lambda e, c=c: e.tensor_scalar(out=hpad[:, c, 1:4, 0:15], in0=hpad[:, c, 0:3, 256:271], scalar1=flags[:, 0:1], scalar2=None, op0=ALU.mult),
                   reads=[hpad_t[c], flags_t], writes=[hpad_t[c]])
            dve.op(lambda e, c=c: e.tensor_scalar(out=hpad[:, c, 0:3, 271:286], in0=hpad[:, c, 1:4, 15:30], scalar1=flags[:, 0:1], scalar2=None, op0=ALU.mult),
                   reads=[hpad_t[c], flags_t], writes=[hpad_t[c]])
        if stop == "conv1":
            return
        mark(f"conv{l}_taps")
        DmA = ph.sbuf("DmA", [128, 16, 128], BF16)
        DmB = ph.sbuf("DmB", [128, 15, 128], BF16)
        DmA_t = ph.T()
        DmB_t = ph.T()
        v32 = ph.sbuf("cv32", [128, 4, TOK], F32)
        v32_t = [[ph.T() for t in range(NT)] for c in range(4)]
        vbf = ph.sbuf("cvbf", [128, 4, TOK], BF16)
        vsq = ph.sbuf("cvsq", [128, 4, TOK], BF16)
        vbf_t = [[ph.T() for t in range(NT)] for c in range(4)]
        vsq_t = [[ph.T() for t in range(NT)] for c in range(4)]

        def buildA(c):
            dve.op(lambda e: e.tensor_tensor(out=DmA[:], in0=bc_mid(identf[:], 16), in1=bc_last(wdw[:, l, c * 31:c * 31 + 16], 128), op=ALU.mult),
                   reads=[identf_t, wdw_t], writes=[DmA_t])

        def buildB(c):
            dve.op(lambda e: e.tensor_tensor(out=DmB[:], in0=bc_mid(identf[:], 15), in1=bc_last(wdw[:, l, c * 31 + 16:(c + 1) * 31], 128), op=ALU.mult),
                   reads=[identf_t, wdw_t], writes=[DmB_t])

        buildA(0)
        buildB(0)
        for c in range(4):
            bd = cvec[:, l, c:c + 1]
            banks = [nextps() for _ in range(4)]
            for seg in range(4):
                c_ps, c_t = banks[seg]
                for j in range(16):
                    mm(c_ps[:, 0:256], DmA[:, j, :], hpad[:, c, seg, j:j + 256], j == 0, False, [DmA_t, hpad_t[c]], c_t)
            for seg in range(4):
                c_ps, c_t = banks[seg]
                for j in range(16, 31):
                    mm(c_ps[:, 0:256], DmB[:, j - 16, :], hpad[:, c, seg, j:j + 256], False, j == 30, [DmB_t, hpad_t[c]], c_t)
            if c + 1 < 4:
                buildA(c + 1)
            for seg in range(4):
                c_ps, c_t = banks[seg]
                t = seg // 2
                dve.op(lambda e, c=c, seg=seg, c_ps=c_ps, bd=bd: e.tensor_scalar(out=v32[:, c, seg * 256:(seg + 1) * 256], in0=c_ps[:, 0:256], scalar1=bd, scalar2=None, op0=ALU.add),
                       reads=[c_t, cvec_t], writes=[v32_t[c][t]])
            if c + 1 < 4:
                buildB(c + 1)
            for t in range(NT):
                act.op(lambda e, c=c, t=t: e.activation(out=vbf[:, c, tl(t)], in_=v32[:, c, tl(t)], func=AF.Copy),
                       reads=[v32_t[c][t]], writes=[vbf_t[c][t]])
                act.op(lambda e, c=c, t=t: e.activation(out=vsq[:, c, tl(t)], in_=v32[:, c, tl(t)], func=AF.Square),
                       reads=[v32_t[c][t]], writes=[vsq_t[c][t]])
        if stop == "conv2":
            return
        tmp1 = ph.sbuf("ctmp", [128, TW], F32)
        tmp1_t = ph.T()
        tmp = [tmp1, tmp1]
        tmp_t = [tmp1_t, tmp1_t]
        rsd = [sgb[0], sgb[1]]
        rsd_t = [sgb_t[0], sgb_t[1]]
        if _os_env("KMARKS"):
            print("SBUF remaining in conv:", nc.sbuf_bytes_remaining)
        for t in range(NT):
            m_ps, m_t = nextps()
            e_ps, e_t = nextps()
            for c in range(4):
                mm(m_ps[:], onesC[:], vbf[:, c, tl(t)], c == 0, c == 3, [onesC_t, vbf_t[c][t]], m_t)
            for c in range(4):
                mm(e_ps[:], onesC[:], vsq[:, c, tl(t)], c == 0, c == 3, [onesC_t, vsq_t[c][t]], e_t)
            act.op(lambda e, t=t, m_ps=m_ps: e.activation(out=tmp[t][:], in_=m_ps[:], func=AF.Square), reads=[m_t], writes=[tmp_t[t]])
            dve.op(lambda e, t=t, e_ps=e_ps: e.tensor_tensor(out=tmp[t][:], in0=e_ps[:], in1=tmp[t][:], op=ALU.subtract), reads=[e_t, tmp_t[t]], writes=[tmp_t[t]])
            act.op(lambda e, t=t: e.activation(out=tmp[t][:], in_=tmp[t][:], func=AF.Ln, bias=epsln[:], scale=1.0), reads=[tmp_t[t], epsln_t], writes=[tmp_t[t]])
            act.op(lambda e, t=t: e.activation(out=rsd[t][:], in_=tmp[t][:], func=AF.Exp, scale=-0.5), reads=[tmp_t[t]], writes=[rsd_t[t]])
            nm_ps, nm_t = nextps()
            r_ps, r_t = nextps()
            dve.op(lambda e, t=t, nm_ps=nm_ps, m_ps=m_ps: e.scalar_tensor_tensor(out=nm_ps[:], in0=m_ps[:], scalar=-1.0, in1=rsd[t][:], op0=ALU.mult, op1=ALU.mult),
                   reads=[m_t, rsd_t[t]], writes=[nm_t])
            act.op(lambda e, t=t, r_ps=r_ps: e.activation(out=r_ps[:], in_=rsd[t][:], func=AF.Copy), reads=[rsd_t[t]], writes=[r_t])
            for c in range(4):
                dve.op(lambda e, c=c, t=t, r_ps=r_ps: e.tensor_tensor(out=v32[:, c, tl(t)], in0=v32[:, c, tl(t)], in1=r_ps[:], op=ALU.mult),
                       reads=[v32_t[c][t], r_t], writes=[v32_t[c][t]])
                dve.op(lambda e, c=c, t=t, nm_ps=nm_ps: e.tensor_tensor(out=v32[:, c, tl(t)], in0=v32[:, c, tl(t)], in1=nm_ps[:], op=ALU.add),
                       reads=[v32_t[c][t], nm_t], writes=[v32_t[c][t]])
                act.op(lambda e, c=c, t=t: e.activation(out=B.hc[:, c, tl(t)], in_=v32[:, c, tl(t)], func=AF.Silu,
                                                        bias=cvec[:, l, 8 + c:9 + c], scale=cvec[:, l, 4 + c:5 + c]),
                       reads=[v32_t[c][t], cvec_t], writes=[B.hc_t[c][t]])
        ph.close()

    def retention(l, B):
        ph = Phase(K)
        qT = ph.sbuf("qT", [128, TOK], BF16)
        kT = ph.sbuf("kT", [128, TOK], BF16)
        qxf = ph.sbuf("qxf", [128, TOK], BF16)
        qxb = ph.sbuf("qxb", [128, TOK], BF16)
        qk_t = [ph.T() for t in range(NT)]
        kzf = ph.sbuf("kzf", [128, 8, 128], BF16)
        kzb = ph.sbuf("kzb", [128, 8, 128], BF16)
        kz_t = [ph.T() for g in range(2)]
        vtok = ph.sbuf("vtok", [128, 8, 256], BF16)
        vtok_t = [ph.T() for n in range(8)]
        srg = ph.sbuf("srg", [128, 2, TOK], BF16)
        srg_t = [[ph.T() for t in range(NT)] for cc in range(2)]
        Sbf = ph.sbuf("Sbf", [128, 2, 8, 256], BF16)
        Sbf_t = [[ph.T() for n in range(8)] for d in range(2)]
        Sseg = ph.sbuf("Sseg", [128, 2, 4, 256], F32)
        Sseg_t = [[ph.T() for s_ in range(4)] for d in range(2)]
        Srun = ph.sbuf("Srun", [128, 2, 256], F32)
        Srun_t = [ph.T() for d in range(2)]
        Sinit = ph.sbuf("Sinit", [128, 2, 256], F32)
        Sinit_t = [ph.T() for d in range(2)]
        PTb = ph.sbuf("PTb", [128, 8, 128], BF16)
        PTb_t = [ph.T() for g in range(2)]
        ontok = ph.sbuf("ontok", [128, 8, 256], BF16)
        ontok_t = [ph.T() for n in range(8)]
        if _os_env("KMARKS"):
            print("SBUF remaining in retention:", nc.sbuf_bytes_remaining)
        stt = ph.sbuf("stt", [128, 8, 16], F32)
        stt_t = [ph.T() for n in range(8)]
        W = I["mix_w_in"][l]
        HV = {}

        def partA(h):
            mark(f"ret{l}_h{h}")
            slot = wslot()
            _, vq, p1 = wload(W, 1024 + h * 128, 128, 8, slot=slot, off=0)
            _, vk, p2 = wload(W, 1536 + h * 128, 128, 8, slot=slot, off=1024)
            _, vv, p3 = wload(W, 2048 + h * 256, 256, 8, slot=slot, off=2048)
            _, vg, p4 = wload(W, 3072 + h * 256, 256, 8, slot=slot, off=4096)
            pool.dma([p1, p2, p3, p4], slot[2], writes=[slot[1]])
            for d in range(2):
                sp.dma((Sinit[:, d, :], I["s0"][l, d, h]), sinit_d[d], writes=[Sinit_t[d]])
            for t in range(NT):
                q_ps, q_t = nextps()
                k_ps, k_t = nextps()
                for kc in range(8):
                    mm(q_ps[:], vq[:, kc, :], u[:, kc, tl(t)], kc == 0, kc == 7, [slot[1], u_t[kc][t]], q_t)
                for kc in range(8):
                    mm(k_ps[:], vk[:, kc, :], u[:, kc, tl(t)], kc == 0, kc == 7, [slot[1], u_t[kc][t]], k_t)
                dve.op(lambda e, t=t, q_ps=q_ps: e.tensor_copy(out=qT[:, tl(t)], in_=q_ps[:]), reads=[q_t], writes=[qk_t[t]])
                act.op(lambda e, t=t, k_ps=k_ps: e.activation(out=kT[:, tl(t)], in_=k_ps[:], func=AF.Copy), reads=[k_t], writes=[qk_t[t]])
                dve.op(lambda e, t=t, q_ps=q_ps, h=h: e.tensor_tensor(out=qxf[:, tl(t)].rearrange("p (a b) -> p a b", b=128),
                                                                      in0=q_ps[:].rearrange("p (a b) -> p a b", b=128), in1=bc_mid(XIf[:, h, :], 4), op=ALU.mult),
                       reads=[q_t, rtab_t], writes=[qk_t[t]])
                dve.op(lambda e, t=t, q_ps=q_ps, h=h: e.tensor_tensor(out=qxb[:, tl(t)].rearrange("p (a b) -> p a b", b=128),
                                                                      in0=q_ps[:].rearrange("p (a b) -> p a b", b=128), in1=bc_mid(XIb[:, h, :], 4), op=ALU.mult),
                       reads=[q_t, rtab_t], writes=[qk_t[t]])
            for n2 in range(4):
                v_ps, v_t = nextps()
                for sub in range(2):
                    n = n2 * 2 + sub
                    for kc in range(8):
                        mm(v_ps[:, sub * 256:(sub + 1) * 256], u[:, kc, n * 128:(n + 1) * 128], vv[:, kc, :], kc == 0, kc == 7, [slot[1], u_t[kc][n // 4]], v_t)
                act.op(lambda e, n2=n2, v_ps=v_ps: e.activation(out=vtok[:, 2 * n2:2 * n2 + 2, :], in_=v_ps[:].rearrange("p (a b) -> p a b", b=256), func=AF.Copy),
                       reads=[v_t], writes=[vtok_t[2 * n2], vtok_t[2 * n2 + 1]])
            for g in range(2):
                t_ps, t_t = nextps()
                for sub in range(4):
                    n = g * 4 + sub
                    mm(t_ps[:, sub * 128:(sub + 1) * 128], kT[:, n * 128:(n + 1) * 128], identb[:], True, True, [qk_t[g], identb_t], t_t)
                act.op(lambda e, g=g, t_ps=t_ps, h=h: e.activation(out=kzf[:, 4 * g:4 * g + 4, :], in_=t_ps[:].rearrange("p (a b) -> p a b", b=128), func=AF.Identity,
                                                                   scale=rsm[:, 8 + h:9 + h]),
                       reads=[t_t, rtab_t], writes=[kz_t[g]])
                act.op(lambda e, g=g, t_ps=t_ps, h=h: e.activation(out=kzb[:, 4 * g:4 * g + 4, :], in_=t_ps[:].rearrange("p (a b) -> p a b", b=128), func=AF.Identity,
                                                                   scale=rsm[:, 12 + h:13 + h]),
                       reads=[t_t, rtab_t], writes=[kz_t[g]])
            HV[h] = (slot, vg)

        def partB(h):
            slot, vg = HV[h]
            mark(f"ret{l}_h{h}_state")
            orders = [list(range(8)), list(range(7, -1, -1))]
            prevs = [(Sinit[:, d, :], Sinit_t[d]) for d in range(2)]
            for step in range(8):
                for d in range(2):
                    n = orders[d][step]
                    kz = kzf if d == 0 else kzb
                    prev, prev_t = prevs[d]
                    cflag = (n in (2, 4, 6)) if d == 0 else (n in (5, 3, 1))
                    if cflag:
                        act.op(lambda e, d=d, n=n, prev=prev: e.activation(out=Sbf[:, d, n, :], in_=prev, func=AF.Identity, scale=flags[:, 0:1]),
                               reads=[prev_t, flags_t], writes=[Sbf_t[d][n]])
                    else:
                        act.op(lambda e, d=d, n=n, prev=prev: e.activation(out=Sbf[:, d, n, :], in_=prev, func=AF.Copy),
                               reads=[prev_t], writes=[Sbf_t[d][n]])
                    kv_ps, kv_t = nextps()
                    mm(kv_ps[:, 0:256], kz[:, n, :], vtok[:, n, :], True, True, [kz_t[n // 4], vtok_t[n]], kv_t)
                    to_seg = (n % 2 == 1) if d == 0 else (n % 2 == 0)
                    if to_seg:
                        dst, dst_t = Sseg[:, d, n // 2, :], Sseg_t[d][n // 2]
                    else:
                        dst, dst_t = Srun[:, d, :], Srun_t[d]
                    gcol = (24 if cflag else 16) + d * 4 + h
                    dve.op(lambda e, dst=dst, prev=prev, kv_ps=kv_ps, gcol=gcol: e.scalar_tensor_tensor(out=dst, in0=prev, scalar=rsm[:, gcol:gcol + 1], in1=kv_ps[:, 0:256],
                                                                                                        op0=ALU.mult, op1=ALU.add),
                           reads=[prev_t, kv_t, rtab_t], writes=[dst_t])
                    if to_seg:
                        oname = "o_sf" if d == 0 else "o_sb"
                        out_toks.append(sp.dma((O[oname][l, n // 2, h], Sseg[:, d, n // 2, :]), sseg_d[d][n // 2], reads=[dst_t]))
                    prevs[d] = (dst, dst_t)
            mark(f"ret{l}_h{h}_o")
            for g in range(2):
                s_ps, s_t = nextps()
                for sub in range(4):
                    n = g * 4 + sub
                    mm(s_ps[:, sub * 128:(sub + 1) * 128], kT[:, n * 128:(n + 1) * 128], qT[:, n * 128:(n + 1) * 128], True, True, [qk_t[g]], s_t)
                dve.op(lambda e, g=g, s_ps=s_ps, h=h: e.tensor_tensor(out=PTb[:, 4 * g:4 * g + 4, :], in0=s_ps[:].rearrange("p (a b) -> p a b", b=128),
                                                                      in1=bc_mid(Dcomb[:, h, :], 4), op=ALU.mult),
                       reads=[s_t, rtab_t], writes=[PTb_t[g]])
            for cc in range(2):
                for t in range(NT):
                    g_ps, g_t = nextps()
                    for kc in range(8):
                        mm(g_ps[:], vg[:, kc, cc * 128:(cc + 1) * 128], u[:, kc, tl(t)], kc == 0, kc == 7, [slot[1], u_t[kc][t]], g_t)
                    act.op(lambda e, cc=cc, t=t, g_ps=g_ps: e.activation(out=srg[:, cc, tl(t)], in_=g_ps[:], func=AF.Silu), reads=[g_t], writes=[srg_t[cc][t]])

        def partC(h):
            obanks = [nextps() for _ in range(4)]
            for n in range(8):
                o_ps, o_t = obanks[n // 2]
                oc = slice((n % 2) * 256, (n % 2) * 256 + 256)
                mm(o_ps[:, oc], PTb[:, n, :], vtok[:, n, :], True, False, [PTb_t[n // 4], vtok_t[n]], o_t)
                mm(o_ps[:, oc], qxf[:, n * 128:(n + 1) * 128], Sbf[:, 0, n, :], False, False, [qk_t[n // 4], Sbf_t[0][n]], o_t)
                mm(o_ps[:, oc], qxb[:, n * 128:(n + 1) * 128], Sbf[:, 1, n, :], False, True, [qk_t[n // 4], Sbf_t[1][n]], o_t)
            for n in range(8):
                o_ps, o_t = obanks[n // 2]
                oc = slice((n % 2) * 256, (n % 2) * 256 + 256)
                dve.op(lambda e, n=n, o_ps=o_ps, oc=oc: e.bn_stats(out=stt[:, n, 0:6], in_=o_ps[:, oc]), reads=[o_t], writes=[stt_t[n]])
                dve.op(lambda e, n=n: e.bn_aggr(out=stt[:, n, 8:10], in_=stt[:, n, 0:6]), reads=[stt_t[n]], writes=[stt_t[n]])
            act.op(lambda e: e.activation(out=stt[:, :, 10], in_=stt[:, :, 9], func=AF.Sqrt, bias=epsln[:], scale=1.0), reads=stt_t + [epsln_t], writes=stt_t)
            dve.op(lambda e: e.reciprocal(out=stt[:, :, 11], in_=stt[:, :, 10]), reads=stt_t, writes=stt_t)
            for n in range(8):
                o_ps, o_t = obanks[n // 2]
                oc = slice((n % 2) * 256, (n % 2) * 256 + 256)
                dve.op(lambda e, n=n, o_ps=o_ps, oc=oc: e.tensor_scalar(out=ontok[:, n, :], in0=o_ps[:, oc], scalar1=stt[:, n, 8:9], scalar2=stt[:, n, 11:12],
                                                                        op0=ALU.subtract, op1=ALU.mult),
                       reads=[o_t, stt_t[n]], writes=[ontok_t[n]])

        def partD(h):
            for cc in range(2):
                for t in range(NT):
                    r_ps, r_t = nextps()
                    for sub in range(4):
                        n = t * 4 + sub
                        mm(r_ps[:, sub * 128:(sub + 1) * 128], ontok[:, n, cc * 128:(cc + 1) * 128], identb[:], True, True, [ontok_t[n], identb_t], r_t)
                    dve.op(lambda e, cc=cc, t=t, r_ps=r_ps, h=h: e.tensor_tensor(out=B.ogT[:, h * 2 + cc, tl(t)], in0=r_ps[:], in1=srg[:, cc, tl(t)], op=ALU.mult),
                           reads=[r_t, srg_t[cc][t]], writes=[B.ogT_t[h * 2 + cc][t]])


        partA(0)
        for h in range(4):
            partB(h)
            partC(h)
            if h + 1 < 4:
                partA(h + 1)
            partD(h)
        ph.close()

    def mla(l, B):
        W = I["mix_w_in"][l]
        SCALE = 96.0 ** -0.5
        mark(f"mlaC1_{l}")
        outer = Phase(K)
        mqn = outer.sbuf("mqn", [128, 4, TOK], BF16)
        mqn_t = [[outer.T() for t in range(NT)] for c in range(4)]
        ckvall = outer.sbuf("ckvall", [128, 2, 1280], BF16)
        ckvall_t = [[outer.T() for kt in range(3)] for c in range(2)]
        krall = outer.sbuf("krall", [96, 1280], BF16)
        krall_t = [outer.T() for kt in range(3)]
        ph = Phase(K)
        x32 = [ph.sbuf("mx32", [128, 4, TW], F32) for _ in range(1)]
        x32_t = [[ph.T() for c in range(4)] for _ in range(1)]
        sq = [ph.sbuf("msq", [128, 4, TW], BF16) for _ in range(1)]
        sq_t = [[ph.T() for c in range(4)] for _ in range(1)]
        rs = [ph.sbuf("mrs", [128, TW], F32) for _ in range(1)]
        rs_t = [ph.T() for _ in range(1)]
        ckv32 = ph.sbuf("ckv32", [128, 2, TOK], F32)
        ckv32_t = [[ph.T() for t in range(NT)] for c in range(2)]
        kr32 = ph.sbuf("kr32", [96, TOK], F32)
        kr32_t = [ph.T() for t in range(NT)]
        rt1 = ph.sbuf("rt1", [96, TW], F32)
        rt2 = ph.sbuf("rt2", [96, TW], F32)
        rt_t = ph.T()
        cst32 = ph.sbuf("cst32", [128, 2, 256], F32)
        cst32_t = ph.T()
        krc32 = ph.sbuf("krc32", [96, 256], F32)
        krc32_t = ph.T()
        if _os_env("KMARKS"):
            print("SBUF remaining in mla C1:", nc.sbuf_bytes_remaining)
        sp.dma((cst32[:], I["ckvT"][l].rearrange("(c p) k -> p c k", p=128)), cch_d, writes=[cst32_t])
        sp.dma((krc32[:], I["krT96"][l]), cch_d, writes=[krc32_t])
        seal_consts([cst32_t, krc32_t], cch_d)
        for c in range(2):
            act.op(lambda e, c=c: e.activation(out=ckvall[:, c, 0:256], in_=cst32[:, c, :], func=AF.Copy), reads=[cst32_t], writes=[ckvall_t[c][0]])
        act.op(lambda e: e.activation(out=krall[64:96, 0:256], in_=krc32[64:96, :], func=AF.Copy), reads=[krc32_t], writes=[krall_t[0]])
        slotA, vA, pA = wload(W, 4096, 512, 8)
        pool.dma([pA], slotA[2], writes=[slotA[1]])
        slot = wslot()
        _, v1, p1 = wload(W, 4608, 288, 8, slot=slot, off=0)
        _, vsw, p2 = wload(I["w_kr96_sw"][l], 0, 96, 8, slot=slot, off=2304)
        pool.dma([p1, p2], slot[2], writes=[slot[1]])
        k = 0
        for t in range(NT):
            def kr_block(t=t):
                kr_ps, kr_t = nextps()
                ks_ps, ks_t = nextps()
                for kc in range(8):
                    mm(kr_ps[0:96, :], v1[:, kc, 192:288], u[:, kc, tl(t)], kc == 0, kc == 7, [slot[1], u_t[kc][t]], kr_t)
                for kc in range(8):
                    mm(ks_ps[0:96, :], vsw[:, kc, :], u[:, kc, tl(t)], kc == 0, kc == 7, [slot[1], u_t[kc][t]], ks_t)
                dve.op(lambda e, t=t, kr_ps=kr_ps: e.tensor_tensor(out=rt1[64:96, :], in0=kr_ps[64:96, :], in1=ropeC[64:96, tl(t)], op=ALU.mult), reads=[kr_t, ropeC_t, rt_t], writes=[rt_t])
                dve.op(lambda e, t=t, ks_ps=ks_ps: e.tensor_tensor(out=rt2[64:96, :], in0=ks_ps[64:96, :], in1=ropeS[64:96, tl(t)], op=ALU.mult), reads=[ks_t, ropeS_t, rt_t], writes=[rt_t])
                dve.op(lambda e, t=t: e.tensor_tensor(out=kr32[64:96, tl(t)], in0=rt1[64:96, :], in1=rt2[64:96, :], op=ALU.add), reads=[rt_t], writes=[kr32_t[t]])
                act.op(lambda e, t=t: e.activation(out=krall[64:96, 256 + t * TW:256 + (t + 1) * TW], in_=kr32[64:96, tl(t)], func=AF.Copy), reads=[kr32_t[t]], writes=[krall_t[1 + t]])

            for (nch, col0, ones_ap, ones_tt, ncol0, is_q) in ((4, 0, onesC, onesC_t, 0, True), (2, 0, onesK, onesK_t, 4, False)):
                b = 0
                wv, wsl = (vA, slotA) if is_q else (v1, slot)
                for c in range(nch):
                    x_ps, x_t = nextps()
                    for kc in range(8):
                        mm(x_ps[:], wv[:, kc, col0 + c * 128:col0 + (c + 1) * 128], u[:, kc, tl(t)], kc == 0, kc == 7, [wsl[1], u_t[kc][t]], x_t)
                    dve.op(lambda e, b=b, c=c, x_ps=x_ps: e.tensor_copy(out=x32[b][:, c, :], in_=x_ps[:]), reads=[x_t], writes=[x32_t[b][c]])
                    act.op(lambda e, b=b, c=c: e.activation(out=sq[b][:, c, :], in_=x32[b][:, c, :], func=AF.Square), reads=[x32_t[b][c]], writes=[sq_t[b][c]])
                if is_q:
                    kr_block()
                m_ps, m_t = nextps()
                for c in range(nch):
                    mm(m_ps[:], ones_ap[:], sq[b][:, c, :], c == 0, c == nch - 1, [ones_tt, sq_t[b][c]], m_t)
                act.op(lambda e, b=b, m_ps=m_ps: e.activation(out=rs[b][:], in_=m_ps[:], func=AF.Ln, bias=epsrms[:], scale=1.0), reads=[m_t, epsrms_t], writes=[rs_t[b]])
                act.op(lambda e, b=b: e.activation(out=rs[b][:], in_=rs[b][:], func=AF.Exp, scale=-0.5), reads=[rs_t[b]], writes=[rs_t[b]])
                for c in range(nch):
                    gcol = mlan[:, l, ncol0 + c:ncol0 + c + 1]
                    if is_q:
                        dve.op(lambda e, b=b, c=c, t=t, gcol=gcol: e.scalar_tensor_tensor(out=mqn[:, c, tl(t)], in0=x32[b][:, c, :], scalar=gcol, in1=rs[b][:],
                                                                                          op0=ALU.mult, op1=ALU.mult),
                               reads=[x32_t[b][c], rs_t[b], mlan_t], writes=[mqn_t[c][t]])
                    else:
                        dve.op(lambda e, b=b, c=c, t=t, gcol=gcol: e.scalar_tensor_tensor(out=ckv32[:, c, tl(t)], in0=x32[b][:, c, :], scalar=gcol, in1=rs[b][:],
                                                                                          op0=ALU.mult, op1=ALU.mult),
                               reads=[x32_t[b][c], rs_t[b], mlan_t], writes=[ckv32_t[c][t]])
                        act.op(lambda e, c=c, t=t: e.activation(out=ckvall[:, c, 256 + t * TW:256 + (t + 1) * TW], in_=ckv32[:, c, tl(t)], func=AF.Copy),
                               reads=[ckv32_t[c][t]], writes=[ckvall_t[c][1 + t]])
        out_toks.append(sp.dma((O["o_ckvT"][l].rearrange("(c p) t -> p c t", p=128), ckv32[:]), ckvo_d, reads=[ckv32_t[c][t] for c in range(2) for t in range(NT)]))
        out_toks.append(sp.dma((O["o_krT"][l], kr32[64:96, :]), kro_d, reads=kr32_t))
        ph.close()
        mark(f"mlaC2_{l}")
        ph = Phase(K)
        VO = ph.sbuf("VO", [128, 10, 8, 128], BF16)
        VO_t = [ph.T() for kc in range(10)]
        dve.op(lambda e: e.memset(VO[:], 1.0), writes=VO_t)
        ET = [ph.sbuf("ET", [128, TW], BF16) for _ in range(3)]
        ET_t = [ph.T() for _ in range(3)]
        qt1 = ph.sbuf("qt1", [96, TW], F32)
        qt2 = ph.sbuf("qt2", [96, TW], F32)
        qt_t = ph.T()
        rden = [ph.sbuf("rden", [128, TW], F32) for _ in range(2)]
        rdst = [ph.sbuf("rdst", [128, TW], F32) for _ in range(2)]
        rdst_t = [ph.T() for _ in range(2)]
        if _os_env("KMARKS"):
            print("SBUF remaining in mla C2:", nc.sbuf_bytes_remaining)
        rden_t = [ph.T() for _ in range(2)]
        slotq = wslot()
        _, vuq, p1 = wload(I["mla_w_uq"][l], 0, 768, 4, slot=slotq, off=0)
        _, vuqs, p2 = wload(I["mla_w_uq_sw"][l], 0, 768, 4, slot=slotq, off=3072)
        pool.dma([p1, p2], slotq[2], writes=[slotq[1]])
        slotk, vkv, p3 = wload(I["mla_w_ukv"][l], 0, 1024, 2)
        pool.dma([p3], slotk[2], writes=[slotk[1]])
        ring_reserved.update([slot_index(slotq), slot_index(slotk)])
        vkv4 = vkv.rearrange("p a (h two d) -> p a h two d", two=2, d=64)
        for kc in range(10):
            v_ps, v_t = nextps()
            kt = 0 if kc < 2 else (1 if kc < 6 else 2)
            for a in range(2):
                mm(v_ps[:], ckvall[:, a, kc * 128:(kc + 1) * 128], vkv4[:, a, :, 1, :], a == 0, a == 1, [slotk[1], ckvall_t[a][kt]], v_t)
            v4 = v_ps[:].rearrange("p (hp two d) -> p hp two d", two=2, d=64)
            VOk = VO[:, kc, :, :].rearrange("p (hp two) c -> p hp two c", two=2)
            act.op(lambda e, v4=v4, VOk=VOk: e.activation(out=VOk[:, :, 0, 0:64], in_=v4[:, :, 0, :], func=AF.Copy), reads=[v_t], writes=[VO_t[kc]])
            act.op(lambda e, v4=v4, VOk=VOk: e.activation(out=VOk[:, :, 1, 64:128], in_=v4[:, :, 1, :], func=AF.Copy), reads=[v_t], writes=[VO_t[kc]])
        pj_i = [0]
        bank7_free = (not ADA_FINS) and all(v == 2 for v in ada_done.values())
        pj_banks = (2, 3, 7) if bank7_free else (2, 3)
        if _os_env("KMARKS"):
            print("bank7_free", l, bank7_free)

        def projps():
            i = pj_banks[pj_i[0] % len(pj_banks)]
            pj_i[0] += 1
            return PS[i], PS_t[i]

        def proj_pieces(h):
            b = h % 2
            pieces = []
            for kt, (k0, kw) in enumerate(((0, 256), (256, 512), (768, 512))):
                def knope(kt=kt, k0=k0, kw=kw):
                    n_ps, n_t = projps()
                    for a in range(2):
                        mm(n_ps[0:64, 0:kw], vkv[:, a, h * 128:h * 128 + 64], ckvall[:, a, k0:k0 + kw], a == 0, a == 1, [slotk[1], ckvall_t[a][kt]], n_t)
                    act.op(lambda e: e.activation(out=KTa[b][0:64, k0:k0 + kw], in_=n_ps[0:64, 0:kw], func=AF.Copy), reads=[n_t], writes=[KTa_t[b]])
                    if kt == 0:
                        dve.op(lambda e: e.tensor_copy(out=KTa[b][64:96, :], in_=krall[64:96, :]), reads=krall_t, writes=[KTa_t[b]])
                pieces.append(knope)
            for t in range(NT):
                def qproj(t=t):
                    q_ps, q_t = projps()
                    qs_ps, qs_t = projps()
                    for a in range(4):
                        mm(q_ps[0:96, :], vuq[:, a, h * 96:(h + 1) * 96], mqn[:, a, tl(t)], a == 0, a == 3, [slotq[1], mqn_t[a][t]], q_t)
                    for a in range(4):
                        mm(qs_ps[0:96, :], vuqs[:, a, h * 96:(h + 1) * 96], mqn[:, a, tl(t)], a == 0, a == 3, [slotq[1], mqn_t[a][t]], qs_t)
                    dve.op(lambda e: e.tensor_tensor(out=qt1[:], in0=q_ps[0:96, :], in1=ropeC[:, tl(t)], op=ALU.mult), reads=[q_t, ropeC_t, qt_t], writes=[qt_t])
                    dve.op(lambda e: e.tensor_tensor(out=qt2[:], in0=qs_ps[0:96, :], in1=ropeS[:, tl(t)], op=ALU.mult), reads=[qs_t, ropeS_t, qt_t], writes=[qt_t])
                    dve.op(lambda e: e.tensor_tensor(out=QTa[b][0:96, tl(t)], in0=qt1[:], in1=qt2[:], op=ALU.add), reads=[qt_t], writes=[QTa_t[b]])
                pieces.append(qproj)
            return pieces

        def proj(h):
            for p_ in proj_pieces(h):
                p_()

        nxt_pieces = []

        ekc = [0]

        def attn(h):
            b = h % 2
            mark(f"mla{l}_h{h}")
            po = (h % 2) * 64
            dn = 64 - po
            for t in range(NT):
                it = h * NT + t
                ob = it % 2
                o_ps, o_t = PS[ob], PS_t[ob]
                SK = 2
                rs_ = {}
                for kc in range(10 + SK):
                    if kc < 10:
                        sb_ = 4 + (ekc[0] % 3)
                        s_ps, s_t = PS[sb_], PS_t[sb_]
                        mm(s_ps[:], KTa[b][:, kc * 128:(kc + 1) * 128], QTa[b][:, tl(t)], True, True, [KTa_t[b], QTa_t[b]], s_t)
                        r = ekc[0] % 3
                        ekc[0] += 1
                        rs_[kc] = r
                        act.op(lambda e, r=r, s_ps=s_ps: e.activation(out=ET[r][:], in_=s_ps[:], func=AF.Exp, scale=SCALE), reads=[s_t], writes=[ET_t[r]])
                        if nxt_pieces and kc in ((2, 5, 8) if t == 0 else (2, 6)):
                            nxt_pieces.pop(0)()
                    pk = kc - SK
                    if pk >= 0:
                        pr = rs_[pk]
                        mm(o_ps[:], VO[:, pk, h, :], ET[pr][:], pk == 0, pk == 9, [VO_t[pk], ET_t[pr]], o_t)
                rb = it % 2
                if fin_q:
                    finish_one()
                dve.op(lambda e, rb=rb, o_ps=o_ps, dn=dn: e.reciprocal(out=rden[rb][dn:dn + 64, :], in_=o_ps[dn:dn + 64, :]), reads=[o_t], writes=[rden_t[rb]])
                sp.dma((rdst[rb][po:po + 64, :], rden[rb][dn:dn + 64, :]), rd_d[rb], reads=[rden_t[rb]], writes=[rdst_t[rb]])
                fin_q.append((rb, o_ps, o_t, po, h, t))

        fin_q = []

        def finish_one():
            rb, o_ps, o_t, po, h, t = fin_q.pop(0)
            dve.op(lambda e: e.tensor_tensor(out=B.attnT[po:po + 64, h // 2, tl(t)], in0=o_ps[po:po + 64, :], in1=rdst[rb][po:po + 64, :], op=ALU.mult),
                   reads=[o_t, rdst_t[rb]], writes=[B.attnT_t[h // 2][t]])

        proj(0)
        for h in range(8):
            if h + 1 < 8:
                nxt_pieces.extend(proj_pieces(h + 1))
            attn(h)
            while nxt_pieces:
                nxt_pieces.pop(0)()
        while fin_q:
            finish_one()
        ring_reserved.clear()
        ph.close()
        outer.close()

    def merge(l, B, post=None):
        W = I["mix_w_in"][l]
        mark(f"merge{l}")
        ph = Phase(K)
        merged = ph.sbuf("merged", [128, 8, TOK], BF16)
        merged_t = [[ph.T() for t in range(NT)] for c in range(8)]
        ph2 = Phase(K)
        sg = [[ph2.sbuf("msg", [128, TW], F32) for b in range(3)] for _ in range(2)]
        sg_t = [[ph2.T() for b in range(3)] for _ in range(2)]
        mt = [[ph2.sbuf("mmt", [128, TW], F32) for b in range(3)] for _ in range(2)]
        mt_t = [[ph2.T() for b in range(3)] for _ in range(2)]
        if _os_env("KMARKS"):
            print("SBUF remaining in merge:", nc.sbuf_bytes_remaining)
        k = 0
        for c in range(8):
            slot = wslot()
            _, vc, p1 = wload(I["conv_w_out"][l], c * 128, 128, 4, slot=slot, off=0)
            _, vr, p2 = wload(I["ret_w_out"][l], c * 128, 128, 8, slot=slot, off=512)
            _, vm, p3 = wload(I["mla_w_out"][l], c * 128, 128, 4, slot=slot, off=1536)
            vg = []
            pg = []
            for b in range(3):
                _, v_, p_ = wload(W, 4896 + b * 1024 + c * 128, 128, 8, slot=slot, off=2048 + b * 1024)
                vg.append(v_)
                pg.append(p_)
            pool.dma([p1, p2, p3] + pg, slot[2], writes=[slot[1]])
            for t in range(NT):
                kb = k % 2
                k += 1
                br = []
                for (vw, nk, src, src_t) in ((vc, 4, B.hc, B.hc_t), (vr, 8, B.ogT, B.ogT_t), (vm, 4, B.attnT, B.attnT_t)):
                    b_ps, b_t = nextps()
                    for a in range(nk):
                        mm(b_ps[:], vw[:, a, :], src[:, a, tl(t)], a == 0, a == nk - 1, [slot[1], src_t[a][t]], b_t)
                    br.append((b_ps, b_t))
                for b in range(3):
                    g_ps, g_t = nextps()
                    for a in range(8):
                        mm(g_ps[:], vg[b][:, a, :], u[:, a, tl(t)], a == 0, a == 7, [slot[1], u_t[a][t]], g_t)
                    act.op(lambda e, kb=kb, b=b, g_ps=g_ps: e.activation(out=sg[kb][b][:], in_=g_ps[:], func=AF.Sigmoid), reads=[g_t], writes=[sg_t[kb][b]])
                for b in range(3):
                    b_ps, b_t = br[b]
                    dve.op(lambda e, kb=kb, b=b, b_ps=b_ps: e.tensor_tensor(out=mt[kb][b][:], in0=b_ps[:], in1=sg[kb][b][:], op=ALU.mult),
                           reads=[b_t, sg_t[kb][b]], writes=[mt_t[kb][b]])
                dve.op(lambda e, kb=kb: e.tensor_tensor(out=mt[kb][0][:], in0=mt[kb][0][:], in1=mt[kb][1][:], op=ALU.add),
                       reads=[mt_t[kb][0], mt_t[kb][1]], writes=[mt_t[kb][0]])
                dve.op(lambda e, kb=kb, c=c, t=t: e.tensor_tensor(out=merged[:, c, tl(t)], in0=mt[kb][0][:], in1=mt[kb][2][:], op=ALU.add),
                       reads=[mt_t[kb][0], mt_t[kb][2]], writes=[merged_t[c][t]])
        ph2.close()
        mark(f"mixo{l}")
        for c in range(8):
            if c % 4 == 0:
                slot, view, pair = wload(I["mix_w_o"][l], (c // 4) * 512, 512, 8)
                pool.dma([pair], slot[2], writes=[slot[1]])
            cc = c % 4
            for t in range(NT):
                o_ps, o_t = nextps()
                for a in range(8):
                    mm(o_ps[:], view[:, a, cc * 128:(cc + 1) * 128], merged[:, a, tl(t)], a == 0, a == 7, [slot[1], merged_t[a][t]], o_t)
                dve.op(lambda e, o_ps=o_ps, c=c, t=t: e.scalar_tensor_tensor(out=xs[:, c, tl(t)], in0=o_ps[:], scalar=tabG[:, l, 8 + c:8 + c + 1],
                                                                             in1=xs[:, c, tl(t)], op0=ALU.mult, op1=ALU.add),
                       reads=[o_t, tab_t[l][1], xs_t[c][t]], writes=[xs_t[c][t]])
        if post is not None:
            post()
        ph.close()

    for j in range(3):
        ada(0, j)
    for c in range(8):
        for t in range(NT):
            act.op(lambda e, c=c, t=t: e.activation(out=u[:, c, tl(t)], in_=xs[:, c, tl(t)], func=AF.Identity,
                                                    bias=tabQ[:, 0, c:c + 1], scale=tabP[:, 0, c:c + 1]),
                   reads=[xs_t[c][t], tab_t[0][0]], writes=[u_t[c][t]])
            dve.op(lambda e, c=c, t=t: e.tensor_scalar(out=xs[:, c, tl(t)], in0=xs[:, c, tl(t)], scalar1=ALPHA, scalar2=None, op0=ALU.mult),
                   reads=[xs_t[c][t]], writes=[xs_t[c][t]])

    def dump(buf, buf_t, nch):
        for c in range(nch):
            for t in range(NT):
                act.op(lambda e, c=c, t=t: e.activation(out=xs[:, c, tl(t)], in_=buf[:, c, tl(t)], func=AF.Copy), reads=[buf_t[c][t]], writes=[xs_t[c][t]])

    for l in range(DEPTH):
        last = (l == DEPTH - 1)
        ffn(l, I["ffn1_w_in"][l], I["ffn1_w_out"][l], 0, hooks=ret_table_parts(l), nhost_tail=(1 if l == 0 else 0),
            post=lambda: layer_norm(l, 0))
        if stop in ("ln1", "rt"):
            break
        mx = Phase(K)
        B = mixer_bufs(mx)
        conv_branch(l, B)
        if stop in ("conv1", "conv2"):
            break
        if stop == "conv":
            dump(B.hc, B.hc_t, 4)
            break
        retention(l, B)
        if stop == "ret":
            dump(B.ogT, B.ogT_t, 8)
            break
        mla(l, B)
        if stop == "mla":
            dump(B.attnT, B.attnT_t, 4)
            break
        merge(l, B, post=lambda: layer_norm(l, 1))
        mx.close()
        if stop == "mix":
            break
        ffn(l, I["ffn2_w_in"][l], I["ffn2_w_out"][l], 2,
            post=lambda: layer_norm(l, 2, final=last))
        if stop == "l0":
            break

    out_d = K.dsem("out")
    yT = O["yT"].rearrange("(c p) t -> p c t", p=128)
    toks = []
    for c in range(8):
        toks.append(sp.dma((yT[:, c, :], xs[:, c, :]), out_d, reads=[xs_t[c][0], xs_t[c][1]]))
    mark("end")
    K.finish(toks + out_toks)
    import os as _os
    if _os.environ.get("KMARKS"):
        import json as _json
        _json.dump(MARKS, open(_os.environ["KMARKS"], "w"))
    st_holder.append(st)
    return nc


st_holder = []

INPUT_SHAPES = {
    "xT": (D, TOK),
    "cond": (128, 8),
    "ada_w": (DEPTH, D, 9 * D),
    "ada_bL": (DEPTH, 128, 72),
    "lngL": (DEPTH, 128, 24),
    "lnbL": (DEPTH, 128, 24),
    "ffn1_w_in": (DEPTH, D, 2 * DFF),
    "ffn1_w_out": (DEPTH, DFF, D),
    "ffn2_w_in": (DEPTH, D, 2 * DFF),
    "ffn2_w_out": (DEPTH, DFF, D),
    "mix_w_in": (DEPTH, D, MIXW),
    "w_kr96_sw": (DEPTH, D, 96),
    "conv_wdwL": (DEPTH, 128, 4, 31),
    "conv_vecL": (DEPTH, 128, 12),
    "conv_w_out": (DEPTH, 512, D),
    "ret_decayL": (DEPTH, 128, 8),
    "ret_w_out": (DEPTH, D, D),
    "mla_normL": (DEPTH, 128, 6),
    "mla_w_uq": (DEPTH, 512, 768),
    "mla_w_uq_sw": (DEPTH, 512, 768),
    "mla_w_ukv": (DEPTH, 256, 1024),
    "mla_w_out": (DEPTH, 512, D),
    "mix_w_o": (DEPTH, D, D),
    "ident": (128, 128),
    "retc": (6, 128, 128),
    "colz": (128, 2),
    "flags": (128, 4),
    "ropeC": (96, TOK),
    "ropeS": (96, TOK),
    "qmask": (32, TOK),
    "kmask": (32, 1280),
    "ckvT": (DEPTH, 256, 256),
    "krT96": (DEPTH, 96, 256),
    "s0": (DEPTH, 2, 4, 128, 256),
}
OUTPUT_SHAPES = {
    "yT": (D, TOK),
    "o_ckvT": (DEPTH, 256, TOK),
    "o_krT": (DEPTH, 32, TOK),
    "o_sf": (DEPTH, 4, 4, 128, 256),
    "o_sb": (DEPTH, 4, 4, 128, 256),
}


def _chunkT(v, n):
    return np.ascontiguousarray(v.reshape(n, 128).T)


def _rope_tables(sample):
    C = np.ones((96, TOK), np.float32)
    S = np.zeros((96, TOK), np.float32)
    if sample:
        t = np.arange(TOK)
        row = (t // 64).astype(np.float32)
        col = (t % 64).astype(np.float32)
        inv = (np.float32(10000.0) ** (-np.arange(8, dtype=np.float32) / np.float32(8.0))).astype(np.float32)
        ang = np.stack([row[:, None] * inv[None, :], col[:, None] * inv[None, :]], axis=1).astype(np.float32)
        cs = np.cos(ang).astype(np.float32)
        sn = np.sin(ang).astype(np.float32)
        for axis in range(2):
            for half in range(2):
                r0 = 64 + axis * 16 + half * 8
                C[r0:r0 + 8, :] = cs[:, axis, :].T
                S[r0:r0 + 8, :] = (-sn[:, axis, :].T if half == 0 else sn[:, axis, :].T)
    return C, S


def prepare_inputs(inp):
    f = lambda a: np.ascontiguousarray(np.asarray(a, dtype=np.float32))
    g = lambda k: np.asarray(inp[k], dtype=np.float32)
    mix = g("mix_w_in")
    perm32 = np.arange(32) ^ 8
    perm96 = np.concatenate([np.arange(64), 64 + perm32])
    uq = g("mla_w_uq")
    uq_sw = uq.reshape(DEPTH, 512, 8, 96)[:, :, :, perm96].reshape(DEPTH, 512, 768)
    kr_sw = np.concatenate([mix[:, :, 4800:4864], mix[:, :, 4864 + perm32]], axis=2)
    j = np.arange(128)[:, None].astype(np.float32)
    i = np.arange(128)[None, :].astype(np.float32)
    ks = np.float32(128.0 ** -0.5)
    retc = np.stack([np.maximum(i - j, 0), np.maximum(j - i, 0), (i >= j) * ks, (j > i) * ks,
                     np.broadcast_to(i + 1, (128, 128)), np.broadcast_to(128 - i, (128, 128))]).astype(np.float32)
    colz = np.stack([127 - np.arange(128), np.arange(128)], axis=1).astype(np.float32)
    dec = np.concatenate([g("ret_decay_fwd"), g("ret_decay_bwd")], axis=1)
    conv_vec = np.stack([np.concatenate([_chunkT(g("conv_b_dw")[l], 4), _chunkT(g("conv_ln_g")[l], 4), _chunkT(g("conv_ln_b")[l], 4)], axis=1)
                         for l in range(DEPTH)])
    mla_norm = np.stack([np.concatenate([_chunkT(g("mla_q_norm")[l], 4), _chunkT(g("mla_kv_norm")[l], 2)], axis=1) for l in range(DEPTH)])
    wdw = g("conv_w_dw")
    wdwL = wdw.reshape(DEPTH, 31, 4, 128).transpose(0, 3, 2, 1)
    tseg = np.arange(TOK) // 256
    qmask = np.zeros((32, TOK), np.float32)
    qmask[0] = 1.0
    for s_ in range(4):
        qmask[1 + s_] = (tseg == s_)
    BIG = 30000.0
    kb = np.arange(1280) // 256
    kmask_p = np.zeros((32, 1280), np.float32)
    kmask_p[0] = -BIG
    for s_ in range(4):
        kmask_p[1 + s_] = BIG * (kb == s_ + 1)
    shared = {
        "ada_w": f(inp["ada_w"]),
        "ada_bL": f(np.stack([_chunkT(g("ada_b")[l], 72) for l in range(DEPTH)])),
        "lngL": f(np.stack([_chunkT(g("post_ln_g")[l].reshape(-1), 24) for l in range(DEPTH)])),
        "lnbL": f(np.stack([_chunkT(g("post_ln_b")[l].reshape(-1), 24) for l in range(DEPTH)])),
        "ffn1_w_in": f(inp["ffn1_w_in"]), "ffn1_w_out": f(inp["ffn1_w_out"]),
        "ffn2_w_in": f(inp["ffn2_w_in"]), "ffn2_w_out": f(inp["ffn2_w_out"]),
        "mix_w_in": f(mix), "w_kr96_sw": f(kr_sw),
        "conv_wdwL": f(wdwL), "conv_vecL": f(conv_vec), "conv_w_out": f(inp["conv_w_out"]),
        "ret_decayL": f(np.broadcast_to(dec[:, None, :], (DEPTH, 128, 8))), "ret_w_out": f(inp["ret_w_out"]),
        "mla_normL": f(mla_norm), "mla_w_uq": f(uq), "mla_w_uq_sw": f(uq_sw),
        "mla_w_ukv": f(inp["mla_w_ukv"]), "mla_w_out": f(inp["mla_w_out"]), "mix_w_o": f(inp["mix_w_o"]),
        "ident": f(np.eye(128)), "retc": f(retc), "colz": f(colz), "qmask": f(qmask),
    }
    ropes = {False: _rope_tables(False), True: _rope_tables(True)}
    maps = []
    for core in range(8):
        m = dict(shared)
        sample = core >= 4
        if not sample:
            x = np.asarray(inp["x_prompt"][4 * core:4 * core + 4]).reshape(TOK, D)
            cv = np.asarray(inp["c_ctx"])
            m["ckvT"] = np.zeros((DEPTH, 256, 256), np.float32)
            m["krT96"] = np.zeros((DEPTH, 96, 256), np.float32)
            m["s0"] = np.zeros((DEPTH, 2, 4, 128, 256), np.float32)
            m["kmask"] = f(kmask_p)
            m["flags"] = np.zeros((128, 4), np.float32)
        else:
            b = core - 4
            x = np.asarray(inp["x_sample"][b])
            cv = np.asarray(inp["c"][b])
            m["ckvT"] = f(np.transpose(g("cache_mla_ckv")[b], (0, 2, 1)))
            kr = np.zeros((DEPTH, 96, 256), np.float32)
            kr[:, 64:96, :] = np.transpose(g("cache_mla_krope")[b], (0, 2, 1))
            m["krT96"] = kr
            m["s0"] = f(np.stack([g("state_ret_fwd")[b], g("state_ret_bwd")[b]], axis=1))
            m["kmask"] = np.zeros((32, 1280), np.float32)
            fl = np.zeros((128, 4), np.float32)
            fl[:, 0] = 1.0
            m["flags"] = fl
        m["ropeC"], m["ropeS"] = ropes[sample]
        m["xT"] = f(x.T)
        m["cond"] = f(_chunkT(cv, 8))
        maps.append(m)
    return maps


_NC_CACHE = {}


def run(inp, stop=None):
    if stop not in _NC_CACHE:
        _NC_CACHE[stop] = build_program(stop)
    nc = _NC_CACHE[stop]
    maps = prepare_inputs(inp)
    maps = [{k: np.ascontiguousarray(v, dtype=np.float32) for k, v in m.items() if k in INPUT_SHAPES} for m in maps]
    res = run_bass_kernel_spmd(nc, maps, core_ids=list(range(8)))
    return res.results


def kernel(**inputs):
    res = run(inputs)
    y_prompt = np.zeros((16, 256, D), np.float32)
    y_sample = np.zeros((4, 1024, D), np.float32)
    n_ckv = np.zeros((16, DEPTH, 256, 256), np.float32)
    n_kr = np.zeros((16, DEPTH, 256, 32), np.float32)
    n_sf = np.zeros((16, DEPTH, 4, 128, 256), np.float32)
    n_sb = np.zeros((16, DEPTH, 4, 128, 256), np.float32)
    for core in range(8):
        r = res[core]
        y = np.asarray(r["yT"]).T
        if core < 4:
            y_prompt[4 * core:4 * core + 4] = y.reshape(4, 256, D)
            ck = np.asarray(r["o_ckvT"])
            kr = np.asarray(r["o_krT"])
            sf = np.asarray(r["o_sf"])
            sb = np.asarray(r["o_sb"])
            for s_ in range(4):
                q = 4 * core + s_
                n_ckv[q] = np.transpose(ck[:, :, s_ * 256:(s_ + 1) * 256], (0, 2, 1))
                n_kr[q] = np.transpose(kr[:, :, s_ * 256:(s_ + 1) * 256], (0, 2, 1))
                n_sf[q] = sf[:, s_]
                n_sb[q] = sb[:, s_]
        else:
            y_sample[core - 4] = y
    return (y_prompt, y_sample, n_ckv, n_kr, n_sf, n_sb)
```

```python
import numpy as np
from contextlib import ExitStack
import concourse.bass as bass
import concourse.mybir as mybir
from concourse.bass_utils import run_bass_kernel_spmd

F32 = mybir.dt.float32
BF16 = mybir.dt.bfloat16
AF = mybir.ActivationFunctionType
ALU = mybir.AluOpType
AX = mybir.AxisListType


class T:
    __slots__ = ("name", "w", "r")

    def __init__(self, name=""):
        self.name = name
        self.w = None
        self.r = []


class DSem:
    def __init__(self, sem):
        self.sem = sem
        self.cnt = 0


class Eng:
    def __init__(self, K, name, sem):
        self.K = K
        self.name = name
        self.sem = sem
        self.cnt = 0
        self.ops = []
        self.waited = {}
        self.pending = False

    def _wait(self, tok):
        if tok is None:
            return
        s, v, owner = tok
        key = id(s)
        if self.waited.get(key, 0) >= v:
            return
        self.waited[key] = v
        self.ops.append(lambda e, s=s, v=v: e.wait_ge(s, v))

    def _deps(self, reads, writes, is_dma=False):
        for t in reads:
            if t.w is not None:
                self._wait(t.w)
        strict = is_dma or self.name != "pe"
        for t in writes:
            if t.w is not None and (strict or t.w[2] is not self):
                self._wait(t.w)
            for r in t.r:
                if strict or r[2] is not self:
                    self._wait(r)

    @staticmethod
    def _record(tok, reads, writes):
        for t in reads:
            if not t.r or t.r[-1] is not tok and t.r[-1] != tok:
                t.r.append(tok)
        for t in writes:
            t.w = tok
            t.r = []

    def op(self, fn, reads=(), writes=(), signal=True):
        self._deps(reads, writes)
        if not signal:
            self.nosig = getattr(self, "nosig", 0) + 1
            if self.nosig >= 12:
                signal = True
        if signal:
            self.nosig = 0
            self.cnt += 1
            sem = self.sem
            self.ops.append(lambda e, fn=fn, sem=sem: fn(e).then_inc(sem, 1))
            tok = (self.sem, self.cnt, self)
            self.pending = False
        else:
            self.ops.append(lambda e, fn=fn: fn(e))
            tok = (self.sem, self.cnt + 1, self)
            self.pending = True
        self._record(tok, reads, writes)
        return tok

    def dma(self, pairs, dsem, reads=(), writes=(), **kw):
        if not isinstance(pairs, list):
            pairs = [pairs]
        self._deps(reads, writes, is_dma=True)
        s = dsem.sem
        for (out, in_) in pairs:
            dsem.cnt += 16
            self.ops.append(lambda e, out=out, in_=in_, s=s, kw=kw: e.dma_start(out=out, in_=in_, **kw).then_inc(s, 16))
        tok = (s, dsem.cnt, None)
        self._record(tok, reads, writes)
        return tok


class Kern:
    def __init__(self, nc, stack):
        self.nc = nc
        self.stack = stack
        self.nsem = 0
        self.nname = 0
        mk = lambda n: Eng(self, n, self.new_sem(n))
        self.pe = mk("pe")
        self.dve = mk("dve")
        self.act = mk("act")
        self.pool = mk("pool")
        self.sp = mk("sp")

    def new_sem(self, name):
        self.nsem += 1
        return self.stack.enter_context(self.nc.semaphore(f"s{self.nsem}_{name}"))

    def dsem(self, name="d"):
        return DSem(self.new_sem(name))

    def sbuf(self, name, shape, dt, stack=None):
        self.nname += 1
        return (stack or self.stack).enter_context(self.nc.sbuf_tensor(f"{name}_{self.nname}", list(shape), dt))

    def psum(self, name, shape, dt=F32):
        return self.stack.enter_context(self.nc.psum_tensor(name, list(shape), dt))

    def finish(self, final_toks):
        best = {}
        for tok in final_toks:
            k = id(tok[0])
            if k not in best or best[k][1] < tok[1]:
                best[k] = tok
        for tok in best.values():
            self.sp._wait(tok)
        for e in (self.pe, self.dve, self.act, self.pool, self.sp):
            assert not e.pending, e.name
        block = self.stack.enter_context(self.nc.Block())
        K = self

        @block.tensor
        def _(e):
            for f in K.pe.ops:
                f(e)

        @block.vector
        def _(e):
            for f in K.dve.ops:
                f(e)

        @block.scalar
        def _(e):
            for f in K.act.ops:
                f(e)

        @block.gpsimd
        def _(e):
            for f in K.pool.ops:
                f(e)

        @block.sync
        def _(e):
            for f in K.sp.ops:
                f(e)


class Phase:
    def __init__(self, K):
        self.K = K
        self.stack = ExitStack()
        self.ts = []

    def sbuf(self, name, shape, dt):
        return self.K.sbuf(name, shape, dt, stack=self.stack)

    def T(self, name=""):
        t = T(name)
        self.ts.append(t)
        return t

    def close(self):
        toks = {}
        for t in self.ts:
            for tok in ([t.w] if t.w is not None else []) + t.r:
                k = id(tok[0])
                if k not in toks or toks[k][1] < tok[1]:
                    toks[k] = tok
        K = self.K
        for e in (K.pe, K.dve, K.act, K.sp):
            for tok in toks.values():
                e._wait(tok)
        self.stack.close()


D = 1024
TOK = 1024
NT = 2
TW = 512
DFF = 2816
NFC = 22
DEPTH = 2
ALPHA = (2 * DEPTH) ** 0.25
LN_EPS = 1e-5
RMS_EPS = 1e-6
MIXW = 7968
NS = 3
SLOT = 6144


import os as _osm


def _os_env(k):
    return _osm.environ.get(k)


def tl(t):
    return slice(t * TW, (t + 1) * TW)


def build_program(stop=None):
    nc = bass.Bass("TRN2", target_bir_lowering=False)
    st = ExitStack()
    K = Kern(nc, st)
    pe, dve, act, pool, sp = K.pe, K.dve, K.act, K.pool, K.sp

    def din(name, shape):
        return nc.dram_tensor(name, list(shape), F32, kind="ExternalInput").ap()

    def dout(name, shape):
        return nc.dram_tensor(name, list(shape), F32, kind="ExternalOutput").ap()

    I = {}
    for name, shape in INPUT_SHAPES.items():
        I[name] = din(name, shape)
    O = {}
    for name, shape in OUTPUT_SHAPES.items():
        O[name] = dout(name, shape)

    xs = K.sbuf("xs", [128, 8, TOK], F32)
    xs_t = [[T(f"xs{c}_{t}") for t in range(NT)] for c in range(8)]
    u = K.sbuf("u", [128, 8, TOK], BF16)
    u_t = [[T(f"u{c}_{t}") for t in range(NT)] for c in range(8)]
    ring = [K.sbuf(f"ring{i}", [128, SLOT], BF16) for i in range(NS)]
    ring_t = [T(f"ring{i}") for i in range(NS)]
    ring_d = [K.dsem(f"ring{i}") for i in range(NS)]
    ring_i = [0]
    PS = [K.psum(f"ps{i}", [128, 512]) for i in range(8)]
    PS_t = [T(f"ps{i}") for i in range(8)]
    ps_i = [0]
    NPS = 7

    def nextps():
        i = ps_i[0] % NPS
        ps_i[0] += 1
        return PS[i], PS_t[i]

    ring_reserved = set()

    def wslot():
        while True:
            i = ring_i[0] % NS
            ring_i[0] += 1
            if i not in ring_reserved:
                return ring[i], ring_t[i], ring_d[i]

    def slot_index(slot):
        return [k for k in range(NS) if ring[k] is slot[0]][0]

    def wload(wap, col0, ncols, kc_n, slot=None, off=0):
        if slot is None:
            slot = wslot()
        s, s_t, s_d = slot
        view = s[:, off:off + kc_n * ncols].rearrange("p (a b) -> p a b", b=ncols)
        src = wap.rearrange("(a p) n -> p a n", p=128)[:, :, col0:col0 + ncols]
        return slot, view, (view, src)

    MARKS = []
    nmm = [0]

    def mark(name):
        MARKS.append((name, nmm[0]))

    def mm(out, lhsT, rhs, start, stop, reads, out_t):
        nmm[0] += 1
        pe.op(lambda e: e.matmul(out, lhsT, rhs, start=start, stop=stop), reads=reads, writes=[out_t], signal=stop)

    cst_d = K.dsem("cst")

    const_ts = []

    def load_const(name, shape, src, dt=F32, eng=None):
        t = K.sbuf(name, shape, dt)
        tt = T(name)
        (eng or sp).dma((t[:], src), cst_d, writes=[tt])
        const_ts.append(tt)
        return t, tt

    def seal_consts(ts, dsem):
        for tt in ts:
            tt.w = (dsem.sem, dsem.cnt, None)

    onesD = K.sbuf("onesD", [128, 128], BF16)
    onesD_t = T("onesD")
    dve.op(lambda e: e.memset(onesD[:], 1.0 / 1024.0), writes=[onesD_t])
    epsln = K.sbuf("epsln", [128, 1], F32)
    epsln_t = T("epsln")
    dve.op(lambda e: e.memset(epsln[:], LN_EPS), writes=[epsln_t])

    cond, cond_t = load_const("cond", [128, 8], I["cond"])
    adab, adab_t = load_const("adab", [128, DEPTH, 72], I["ada_bL"].rearrange("l p j -> p l j"))
    lng, lng_t = load_const("lng", [128, DEPTH, 24], I["lngL"].rearrange("l p j -> p l j"))
    lnb, lnb_t = load_const("lnb", [128, DEPTH, 24], I["lnbL"].rearrange("l p j -> p l j"))

    identf, identf_t = load_const("identf", [128, 128], I["ident"])
    cM, cM_t = load_const("cM", [128, 6, 128], I["retc"].rearrange("k p i -> p k i"))
    colz, colz_t = load_const("colz", [128, 2], I["colz"])
    flags, flags_t = load_const("flags", [128, 4], I["flags"])
    ropeC, ropeC_t = load_const("ropeC", [96, TOK], I["ropeC"])
    ropeS, ropeS_t = load_const("ropeS", [96, TOK], I["ropeS"])
    wdw, wdw_t = load_const("wdw", [128, DEPTH, 4 * 31], I["conv_wdwL"].rearrange("l p c j -> p l (c j)"))
    cvec, cvec_t = load_const("cvec", [128, DEPTH, 12], I["conv_vecL"].rearrange("l p j -> p l j"))
    rdec, rdec_t = load_const("rdec", [128, DEPTH, 8], I["ret_decayL"].rearrange("l p j -> p l j"))
    mlan, mlan_t = load_const("mlan", [128, DEPTH, 6], I["mla_normL"].rearrange("l p j -> p l j"))
    seal_consts(const_ts, cst_d)

    xs_d = K.dsem("xs")
    xT = I["xT"].rearrange("(c p) t -> p c t", p=128)
    for c in range(8):
        sp.dma((xs[:, c, :], xT[:, c, :]), xs_d, writes=[xs_t[c][0], xs_t[c][1]])
    seal_consts([xs_t[c][t] for c in range(8) for t in range(NT)], xs_d)

    csil = K.sbuf("csil", [128, 8], BF16)
    csil_t = T("csil")
    act.op(lambda e: e.activation(out=csil[:], in_=cond[:], func=AF.Silu), reads=[cond_t], writes=[csil_t])
    identb = K.sbuf("identb", [128, 128], BF16)
    identb_t = T("identb")
    act.op(lambda e: e.activation(out=identb[:], in_=identf[:], func=AF.Copy), reads=[identf_t], writes=[identb_t])

    mod = K.sbuf("mod", [128, DEPTH, 72], F32)
    mod_t = [[T(f"mod{l}_{j}") for j in range(9)] for l in range(DEPTH)]
    tabP = K.sbuf("tabP", [128, DEPTH, 24], F32)
    tabQ = K.sbuf("tabQ", [128, DEPTH, 24], F32)
    tabG = K.sbuf("tabG", [128, DEPTH, 24], F32)
    tabAG = K.sbuf("tabAG", [128, DEPTH, 24], F32)
    tabAB = K.sbuf("tabAB", [128, DEPTH, 24], F32)
    tab_t = [[T(f"tab{l}_{s}") for s in range(3)] for l in range(DEPTH)]
    tabA_t = T("tabA")
    dve.op(lambda e: e.tensor_scalar(out=tabAG[:], in0=lng[:], scalar1=ALPHA, scalar2=None, op0=ALU.mult), reads=[lng_t], writes=[tabA_t])
    dve.op(lambda e: e.tensor_scalar(out=tabAB[:], in0=lnb[:], scalar1=ALPHA, scalar2=None, op0=ALU.mult), reads=[lnb_t], writes=[tabA_t])

    MODPS = PS[7]
    MODPS_t = PS_t[7]

    ada_done = {}

    def ada_half(l, j, half, defer=False):
        slot, view, pair = wload(I["ada_w"][l], j * 1024 + half * 512, 512, 8)
        pool.dma([pair], slot[2], writes=[slot[1]])
        for cc in range(4):
            c = half * 4 + cc
            col = (l * 72 + j * 8 + c)
            for kc in range(8):
                mm(MODPS[:, col:col + 1], view[:, kc, cc * 128:(cc + 1) * 128], csil[:, kc:kc + 1], kc == 0, kc == 7,
                   [slot[1], csil_t], MODPS_t)
        ada_done[(l, j)] = ada_done.get((l, j), 0) + 1
        if ada_done[(l, j)] == 2:
            def fin():
                c0 = l * 72 + j * 8
                dve.op(lambda e: e.tensor_tensor(out=mod[:, l, j * 8:(j + 1) * 8], in0=MODPS[:, c0:c0 + 8], in1=adab[:, l, j * 8:(j + 1) * 8], op=ALU.add),
                       reads=[MODPS_t, adab_t], writes=[mod_t[l][j]])
                if j % 3 == 2:
                    tables(l, j // 3)
            if defer:
                return fin
            fin()
        return None

    def ada(l, j):
        ada_half(l, j, 0)
        ada_half(l, j, 1)

    def ada_jobs(l, s):
        return [(l, j, h) for j in (3 * s, 3 * s + 1, 3 * s + 2) for h in range(2)]

    ADAQ = [job for l_ in range(DEPTH) for s_ in range(3) if (l_, s_) != (0, 0) for job in ada_jobs(l_, s_)]
    ADA_FINS = []

    def host(n):
        for _ in range(n):
            if not ADAQ:
                return
            f_ = ada_half(*ADAQ.pop(0), defer=True)
            if f_ is not None:
                ADA_FINS.append(f_)

    def flush_fins():
        while ADA_FINS:
            ADA_FINS.pop(0)()

    def need(l, s):
        while any((jl, jj // 3) == (l, s) for (jl, jj, jh) in ADAQ):
            host(1)
        flush_fins()

    def tables(l, s):
        sh = mod[:, l, (3 * s) * 8:(3 * s + 1) * 8]
        sc = mod[:, l, (3 * s + 1) * 8:(3 * s + 2) * 8]
        gt = mod[:, l, (3 * s + 2) * 8:(3 * s + 3) * 8]
        P = tabP[:, l, s * 8:(s + 1) * 8]
        Q = tabQ[:, l, s * 8:(s + 1) * 8]
        G = tabG[:, l, s * 8:(s + 1) * 8]
        rd = [mod_t[l][3 * s], mod_t[l][3 * s + 1], mod_t[l][3 * s + 2], lng_t, lnb_t]
        wr = [tab_t[l][s]]
        first = (l == 0 and s == 0)
        dve.op(lambda e: e.tensor_scalar(out=P, in0=sc, scalar1=1.0, scalar2=None, op0=ALU.add), reads=rd, writes=wr)
        if first:
            dve.op(lambda e: e.tensor_copy(out=Q, in_=sh), reads=rd, writes=wr)
        else:
            pl, ps_ = (l, s - 1) if s > 0 else (l - 1, 2)
            gp = lng[:, pl, ps_ * 8:(ps_ + 1) * 8]
            bp = lnb[:, pl, ps_ * 8:(ps_ + 1) * 8]
            dve.op(lambda e: e.tensor_tensor(out=Q, in0=bp, in1=P, op=ALU.mult), reads=rd + wr, writes=wr)
            dve.op(lambda e: e.tensor_tensor(out=Q, in0=Q, in1=sh, op=ALU.add), reads=rd + wr, writes=wr)
            dve.op(lambda e: e.tensor_tensor(out=P, in0=gp, in1=P, op=ALU.mult), reads=rd + wr, writes=wr)
        gs = 1.0 if s == 1 else 0.5
        dve.op(lambda e: e.tensor_scalar(out=G, in0=gt, scalar1=gs, scalar2=None, op0=ALU.mult), reads=rd, writes=wr)

    def ffn(l, w_in, w_out, s, jobs=(), hooks=(), post=None, nhost_tail=0):
        jobs = list(jobs)
        hooks = list(hooks)
        mark(f"ffn{l}_{s}")
        ph = Phase(K)
        hid = ph.sbuf("hid", [128, NFC, TOK], BF16)
        hid_t = [[ph.T() for t in range(NT)] for fc in range(NFC)]
        sgb = [ph.sbuf("sg", [128, TW], F32) for _ in range(3)]
        sgb_t = [ph.T() for _ in range(3)]
        if _os_env("KMARKS"):
            print("SBUF remaining in FFN:", nc.sbuf_bytes_remaining)
        k = 0
        for blk in range(11):
            slot = wslot()
            _, vg, pg = wload(w_in, blk * 256, 256, 8, slot=slot, off=0)
            _, vu, pu = wload(w_in, DFF + blk * 256, 256, 8, slot=slot, off=2048)
            pool.dma([pg, pu], slot[2], writes=[slot[1]])
            for j in range(2):
                fc = blk * 2 + j
                for t in range(NT):
                    g_ps, g_t = nextps()
                    u_ps, up_t = nextps()
                    for kc in range(8):
                        mm(g_ps[:], vg[:, kc, j * 128:(j + 1) * 128], u[:, kc, tl(t)], kc == 0, kc == 7, [slot[1], u_t[kc][t]], g_t)
                    for kc in range(8):
                        mm(u_ps[:], vu[:, kc, j * 128:(j + 1) * 128], u[:, kc, tl(t)], kc == 0, kc == 7, [slot[1], u_t[kc][t]], up_t)
                    sg, sg_t = sgb[k % 3], sgb_t[k % 3]
                    k += 1
                    act.op(lambda e, sg=sg, g_ps=g_ps: e.activation(out=sg[:], in_=g_ps[:], func=AF.Silu), reads=[g_t], writes=[sg_t])
                    dve.op(lambda e, sg=sg, u_ps=u_ps, fc=fc, t=t: e.tensor_tensor(out=hid[:, fc, tl(t)], in0=u_ps[:], in1=sg[:], op=ALU.mult),
                           reads=[up_t, sg_t], writes=[hid_t[fc][t]])
            if blk % 2 == 1:
                host(1)
            if hooks and blk % 2 == 0 and blk >= 2:
                hooks.pop(0)()
        if nhost_tail:
            host(nhost_tail)
        flush_fins()
        while hooks:
            hooks.pop(0)()
        mark(f"ffn{l}_{s}_out")
        for blk in range(4):
            slot, view, pair = wload(w_out, blk * 256, 256, NFC)
            pool.dma([pair], slot[2], writes=[slot[1]])
            for j in range(2):
                c = blk * 2 + j
                for t in range(NT):
                    o_ps, o_t = nextps()
                    for kc in range(NFC):
                        mm(o_ps[:], view[:, kc, j * 128:(j + 1) * 128], hid[:, kc, tl(t)], kc == 0, kc == NFC - 1, [slot[1], hid_t[kc][t]], o_t)
                    dve.op(lambda e, o_ps=o_ps, c=c, t=t: e.scalar_tensor_tensor(out=xs[:, c, tl(t)], in0=o_ps[:], scalar=tabG[:, l, s * 8 + c:s * 8 + c + 1],
                                                                                 in1=xs[:, c, tl(t)], op0=ALU.mult, op1=ALU.add),
                           reads=[o_t, tab_t[l][s], xs_t[c][t]], writes=[xs_t[c][t]])
        if post is not None:
            post()
        ph.close()

    def layer_norm(l, s, final=False, jobs=()):
        nl, ns_ = (l, s + 1) if s < 2 else (l + 1, 0)
        mark(f"ln{l}_{s}")
        ph = Phase(K)
        ybf = [ph.sbuf("ybf", [128, 8, TW], BF16) for _ in range(NT)]
        ysq = [ph.sbuf("ysq", [128, 8, TW], BF16) for _ in range(NT)]
        ybf_t = [ph.T() for _ in range(NT)]
        ysq_t = [ph.T() for _ in range(NT)]
        tmp = [ph.sbuf("lntmp", [128, TW], F32) for _ in range(NT)]
        tmp_t = [ph.T() for _ in range(NT)]
        rsd = [ph.sbuf("lnrsd", [128, TW], F32) for _ in range(NT)]
        rsd_t = [ph.T() for _ in range(NT)]
        if _os_env("KMARKS"):
            print("SBUF remaining in LN:", nc.sbuf_bytes_remaining)
        gtab = lng if final else tabAG
        btab = lnb if final else tabAB
        gb_t = [lng_t, lnb_t] if final else [tabA_t]
        if not final:
            need(nl, ns_)
        for t in range(NT):
            for c in range(8):
                act.op(lambda e, c=c, t=t: e.activation(out=ybf[t][:, c, :], in_=xs[:, c, tl(t)], func=AF.Copy), reads=[xs_t[c][t]], writes=[ybf_t[t]])
                act.op(lambda e, c=c, t=t: e.activation(out=ysq[t][:, c, :], in_=xs[:, c, tl(t)], func=AF.Square), reads=[xs_t[c][t]], writes=[ysq_t[t]])
        stat = []
        for t in range(NT):
            m_ps, m_t = nextps()
            e_ps, e_t = nextps()
            for c in range(8):
                mm(m_ps[:], onesD[:], ybf[t][:, c, :], c == 0, c == 7, [onesD_t, ybf_t[t]], m_t)
            for c in range(8):
                mm(e_ps[:], onesD[:], ysq[t][:, c, :], c == 0, c == 7, [onesD_t, ysq_t[t]], e_t)
            stat.append((m_ps, m_t, e_ps, e_t))
        host(6 if s == 0 else 4)
        nr = []
        for t in range(NT):
            m_ps, m_t, e_ps, e_t = stat[t]
            act.op(lambda e, t=t, m_ps=m_ps: e.activation(out=tmp[t][:], in_=m_ps[:], func=AF.Square), reads=[m_t], writes=[tmp_t[t]])
            dve.op(lambda e, t=t, e_ps=e_ps: e.tensor_tensor(out=tmp[t][:], in0=e_ps[:], in1=tmp[t][:], op=ALU.subtract), reads=[e_t, tmp_t[t]], writes=[tmp_t[t]])
            act.op(lambda e, t=t: e.activation(out=tmp[t][:], in_=tmp[t][:], func=AF.Ln, bias=epsln[:], scale=1.0), reads=[tmp_t[t], epsln_t], writes=[tmp_t[t]])
            act.op(lambda e, t=t: e.activation(out=rsd[t][:], in_=tmp[t][:], func=AF.Exp, scale=-0.5), reads=[tmp_t[t]], writes=[rsd_t[t]])
            nm_ps, nm_t = nextps()
            r_ps, r_t = nextps()
            dve.op(lambda e, t=t, nm_ps=nm_ps, m_ps=m_ps: e.scalar_tensor_tensor(out=nm_ps[:], in0=m_ps[:], scalar=-1.0, in1=rsd[t][:], op0=ALU.mult, op1=ALU.mult),
                   reads=[m_t, rsd_t[t]], writes=[nm_t])
            act.op(lambda e, t=t, r_ps=r_ps: e.activation(out=r_ps[:], in_=rsd[t][:], func=AF.Copy), reads=[rsd_t[t]], writes=[r_t])
            nr.append((nm_ps, nm_t, r_ps, r_t))
        for t in range(NT):
            nm_ps, nm_t, r_ps, r_t = nr[t]
            for c in range(8):
                dve.op(lambda e, c=c, t=t, r_ps=r_ps: e.tensor_tensor(out=xs[:, c, tl(t)], in0=xs[:, c, tl(t)], in1=r_ps[:], op=ALU.mult),
                       reads=[xs_t[c][t], r_t], writes=[xs_t[c][t]])
                dve.op(lambda e, c=c, t=t, nm_ps=nm_ps: e.tensor_tensor(out=xs[:, c, tl(t)], in0=xs[:, c, tl(t)], in1=nm_ps[:], op=ALU.add),
                       reads=[xs_t[c][t], nm_t], writes=[xs_t[c][t]])
                if not final:
                    act.op(lambda e, c=c, t=t: e.activation(out=u[:, c, tl(t)], in_=xs[:, c, tl(t)], func=AF.Identity,
                                                            bias=tabQ[:, nl, ns_ * 8 + c:ns_ * 8 + c + 1], scale=tabP[:, nl, ns_ * 8 + c:ns_ * 8 + c + 1]),
                           reads=[xs_t[c][t], tab_t[nl][ns_]], writes=[u_t[c][t]])
            for c in range(8):
                if c < 4:
                    dve.op(lambda e, c=c, t=t: e.tensor_scalar(out=xs[:, c, tl(t)], in0=xs[:, c, tl(t)], scalar1=gtab[:, l, s * 8 + c:s * 8 + c + 1],
                                                               scalar2=btab[:, l, s * 8 + c:s * 8 + c + 1], op0=ALU.mult, op1=ALU.add),
                           reads=[xs_t[c][t]] + gb_t, writes=[xs_t[c][t]])
                else:
                    act.op(lambda e, c=c, t=t: e.activation(out=xs[:, c, tl(t)], in_=xs[:, c, tl(t)], func=AF.Identity,
                                                            bias=btab[:, l, s * 8 + c:s * 8 + c + 1], scale=gtab[:, l, s * 8 + c:s * 8 + c + 1]),
                           reads=[xs_t[c][t]] + gb_t, writes=[xs_t[c][t]])
        flush_fins()
        ph.close()


    onesC = K.sbuf("onesC", [128, 128], BF16)
    onesC_t = T("onesC")
    dve.op(lambda e: e.memset(onesC[:], 1.0 / 512.0), writes=[onesC_t])
    onesK = K.sbuf("onesK", [128, 128], BF16)
    onesK_t = T("onesK")
    dve.op(lambda e: e.memset(onesK[:], 1.0 / 256.0), writes=[onesK_t])
    ones64 = K.sbuf("ones64", [128, 64], BF16)
    ones64_t = T("ones64")
    dve.op(lambda e: e.memset(ones64[:], 1.0), writes=[ones64_t])
    epsrms = K.sbuf("epsrms", [128, 1], F32)
    epsrms_t = T("epsrms")
    dve.op(lambda e: e.memset(epsrms[:], RMS_EPS), writes=[epsrms_t])
    one1 = K.sbuf("one1", [128, 1], F32)
    one1_t = T("one1")
    dve.op(lambda e: e.memset(one1[:], 1.0), writes=[one1_t])


    QTa = [K.sbuf(f"QTa{b}", [128, TOK], BF16) for b in range(2)]
    QTa_t = [T(f"QTa{b}") for b in range(2)]
    KTa = [K.sbuf(f"KTa{b}", [128, 1280], BF16) for b in range(2)]
    KTa_t = [T(f"KTa{b}") for b in range(2)]
    msk_d = K.dsem("msk")
    for b in range(2):
        pool.dma((QTa[b][96:128, :], I["qmask"]), msk_d, writes=[QTa_t[b]])
        pool.dma((KTa[b][96:128, :], I["kmask"]), msk_d, writes=[KTa_t[b]])
    seal_consts(QTa_t + KTa_t, msk_d)

    class Bundle:
        pass

    def mixer_bufs(ph):
        B = Bundle()
        B.hc = ph.sbuf("hc", [128, 4, TOK], BF16)
        B.hc_t = [[ph.T() for t in range(NT)] for c in range(4)]
        B.ogT = ph.sbuf("ogT", [128, 8, TOK], BF16)
        B.ogT_t = [[ph.T() for t in range(NT)] for c in range(8)]
        B.attnT = ph.sbuf("attnT", [128, 4, TOK], BF16)
        B.attnT_t = [[ph.T() for t in range(NT)] for c in range(4)]
        return B

    Dcomb = K.sbuf("Dcomb", [128, 4, 128], F32)
    XIf = K.sbuf("XIf", [128, 4, 128], F32)
    XIb = K.sbuf("XIb", [128, 4, 128], F32)
    rsm = K.sbuf("rsm", [128, 40], F32)
    rtab_t = T("rtab")
    sinit_d = [K.dsem("sinit") for d in range(2)]
    sseg_d = [[K.dsem("sseg") for s_ in range(4)] for d in range(2)]
    ckvo_d = K.dsem("ckvo")
    kro_d = K.dsem("kro")
    cch_d = K.dsem("cch")
    rd_d = [K.dsem("rd") for _ in range(2)]
    out_toks = []

    def bc_mid(ap2d, n):
        a = ap2d.ap
        return bass.AP(ap2d.tensor, ap2d.offset, [list(a[0]), [0, n], list(a[-1])])

    def bc_last(ap2d, n):
        a = ap2d.ap
        return bass.AP(ap2d.tensor, ap2d.offset, [list(a[0]), list(a[-1]), [0, n]])

    def ret_table_parts(l):
        lg = rsm[:, 0:8]
        tmp8 = rsm[:, 32:40]
        rd = [rdec_t, one1_t, cM_t, colz_t, flags_t, rtab_t]
        wr = [rtab_t]
        kscale = 128.0 ** -0.5

        def pre():
            act.op(lambda e: e.activation(out=tmp8, in_=rdec[:, l, :], func=AF.Exp, scale=-1.0), reads=rd, writes=wr)
            act.op(lambda e: e.activation(out=lg, in_=tmp8, func=AF.Ln, bias=one1[:], scale=1.0), reads=rd, writes=wr)
            dve.op(lambda e: e.tensor_scalar(out=lg, in0=lg, scalar1=-1.0, scalar2=None, op0=ALU.mult), reads=rd, writes=wr)

        def head(h):
            lf = rsm[:, h:h + 1]
            lb = rsm[:, 4 + h:5 + h]
            act.op(lambda e: e.activation(out=Dcomb[:, h, :], in_=cM[:, 0, :], func=AF.Exp, scale=lf), reads=rd, writes=wr)
            dve.op(lambda e: e.tensor_tensor(out=Dcomb[:, h, :], in0=Dcomb[:, h, :], in1=cM[:, 2, :], op=ALU.mult), reads=rd, writes=wr)
            act.op(lambda e: e.activation(out=XIb[:, h, :], in_=cM[:, 1, :], func=AF.Exp, scale=lb), reads=rd, writes=wr)
            dve.op(lambda e: e.tensor_tensor(out=XIb[:, h, :], in0=XIb[:, h, :], in1=cM[:, 3, :], op=ALU.mult), reads=rd, writes=wr)
            dve.op(lambda e: e.tensor_tensor(out=Dcomb[:, h, :], in0=Dcomb[:, h, :], in1=XIb[:, h, :], op=ALU.add), reads=rd, writes=wr)
            act.op(lambda e: e.activation(out=XIf[:, h, :], in_=cM[:, 4, :], func=AF.Exp, scale=lf), reads=rd, writes=wr)
            act.op(lambda e: e.activation(out=XIb[:, h, :], in_=cM[:, 5, :], func=AF.Exp, scale=lb), reads=rd, writes=wr)
            act.op(lambda e: e.activation(out=rsm[:, 8 + h:9 + h], in_=colz[:, 0:1], func=AF.Exp, scale=lf), reads=rd, writes=wr)
            act.op(lambda e: e.activation(out=rsm[:, 12 + h:13 + h], in_=colz[:, 1:2], func=AF.Exp, scale=lb), reads=rd, writes=wr)

        def post():
            dve.op(lambda e: e.tensor_scalar(out=rsm[:, 8:16], in0=rsm[:, 8:16], scalar1=kscale, scalar2=None, op0=ALU.mult), reads=rd, writes=wr)
            act.op(lambda e: e.activation(out=rsm[:, 16:24], in_=lg, func=AF.Exp, scale=128.0), reads=rd, writes=wr)
            dve.op(lambda e: e.tensor_scalar(out=rsm[:, 24:32], in0=rsm[:, 16:24], scalar1=flags[:, 0:1], scalar2=None, op0=ALU.mult), reads=rd, writes=wr)

        return [pre] + [(lambda h=h: head(h)) for h in range(4)] + [post]

    def ret_tables(l):
        for f_ in ret_table_parts(l):
            f_()

    def norm_stats(ph, ps_list, ones_t_ap, ones_tt, sq_list, sq_tt, eps_ap, eps_tt, want_mean):
        raise NotImplementedError

    def conv_branch(l, B):
        mark(f"conv{l}")
        ph = Phase(K)
        hpad = ph.sbuf("hpad", [128, 4, 4, 286], BF16)
        hpad_t = [ph.T() for c in range(4)]
        for c in range(4):
            dve.op(lambda e, c=c: e.memset(hpad[:, c, :, :], 0.0), writes=[hpad_t[c]])
        sgb = [ph.sbuf("csg", [128, TW], F32) for _ in range(2)]
        sgb_t = [ph.T() for _ in range(2)]
        k = 0
        for c in range(4):
            if c % 2 == 0:
                slot = wslot()
                _, va, pa = wload(I["mix_w_in"][l], (c // 2) * 256, 256, 8, slot=slot, off=0)
                _, vgl, pgl = wload(I["mix_w_in"][l], 512 + (c // 2) * 256, 256, 8, slot=slot, off=2048)
                pool.dma([pa, pgl], slot[2], writes=[slot[1]])
            cc = c % 2
            for t in range(NT):
                a_ps, a_t = nextps()
                g_ps, g_t = nextps()
                for kc in range(8):
                    mm(a_ps[:], va[:, kc, cc * 128:(cc + 1) * 128], u[:, kc, tl(t)], kc == 0, kc == 7, [slot[1], u_t[kc][t]], a_t)
                for kc in range(8):
                    mm(g_ps[:], vgl[:, kc, cc * 128:(cc + 1) * 128], u[:, kc, tl(t)], kc == 0, kc == 7, [slot[1], u_t[kc][t]], g_t)
                sg, sg_t = sgb[k % 2], sgb_t[k % 2]
                k += 1
                act.op(lambda e, sg=sg, g_ps=g_ps: e.activation(out=sg[:], in_=g_ps[:], func=AF.Sigmoid), reads=[g_t], writes=[sg_t])
                dve.op(lambda e, sg=sg, a_ps=a_ps, c=c, t=t: e.tensor_tensor(out=hpad[:, c, 2 * t:2 * t + 2, 15:271],
                                                                             in0=a_ps[:].rearrange("p (a b) -> p a b", b=256),
                                                                             in1=sg[:].rearrange("p (a b) -> p a b", b=256), op=ALU.mult),
                       reads=[a_t, sg_t], writes=[hpad_t[c]])
            dve.op(lambda e, c=c: e.tensor_scalar(out=hpad[:, c, 1:4, 0:15], in0=hpad[:, c, 0:3, 256:271], scalar1=flags[:, 0:1], scalar2=None, op0=ALU.mult),
                   reads=[hpad_t[c], flags_t], writes=[hpad_t[c]])
            dve.op(lambda e, c=c: e.tensor_scalar(out=hpad[:, c, 0:3, 271:286], in0=hpad[:, c, 1:4, 15:30], scalar1=flags[:, 0:1], scalar2=None, op0=ALU.mult),
                   reads=[hpad_t[c], flags_t], writes=[hpad_t[c]])
        if stop == "conv1":
            return
        mark(f"conv{l}_taps")
        DmA = ph.sbuf("DmA", [128, 16, 128], BF16)
        DmB = ph.sbuf("DmB", [128, 15, 128], BF16)
        DmA_t = ph.T()
        DmB_t = ph.T()
        v32 = ph.sbuf("cv32", [128, 4, TOK], F32)
        v32_t = [[ph.T() for t in range(NT)] for c in range(4)]
        vbf = ph.sbuf("cvbf", [128, 4, TOK], BF16)
        vsq = ph.sbuf("cvsq", [128, 4, TOK], BF16)
        vbf_t = [[ph.T() for t in range(NT)] for c in range(4)]
        vsq_t = [[ph.T() for t in range(NT)] for c in range(4)]

        def buildA(c):
            dve.op(lambda e: e.tensor_tensor(out=DmA[:], in0=bc_mid(identf[:], 16), in1=bc_last(wdw[:, l, c * 31:c * 31 + 16], 128), op=ALU.mult),
                   reads=[identf_t, wdw_t], writes=[DmA_t])

        def buildB(c):
            dve.op(lambda e: e.tensor_tensor(out=DmB[:], in0=bc_mid(identf[:], 15), in1=bc_last(wdw[:, l, c * 31 + 16:(c + 1) * 31], 128), op=ALU.mult),
                   reads=[identf_t, wdw_t], writes=[DmB_t])

        buildA(0)
        buildB(0)
        for c in range(4):
            bd = cvec[:, l, c:c + 1]
            banks = [nextps() for _ in range(4)]
            for seg in range(4):
                c_ps, c_t = banks[seg]
                for j in range(16):
                    mm(c_ps[:, 0:256], DmA[:, j, :], hpad[:, c, seg, j:j + 256], j == 0, False, [DmA_t, hpad_t[c]], c_t)
            for seg in range(4):
                c_ps, c_t = banks[seg]
                for j in range(16, 31):
                    mm(c_ps[:, 0:256], DmB[:, j - 16, :], hpad[:, c, seg, j:j + 256], False, j == 30, [DmB_t, hpad_t[c]], c_t)
            if c + 1 < 4:
                buildA(c + 1)
            for seg in range(4):
                c_ps, c_t = banks[seg]
                t = seg // 2
                dve.op(lambda e, c=c, seg=seg, c_ps=c_ps, bd=bd: e.tensor_scalar(out=v32[:, c, seg * 256:(seg + 1) * 256], in0=c_ps[:, 0:256], scalar1=bd, scalar2=None, op0=ALU.add),
                       reads=[c_t, cvec_t], writes=[v32_t[c][t]])
            if c + 1 < 4:
                buildB(c + 1)
            for t in range(NT):
                act.op(lambda e, c=c, t=t: e.activation(out=vbf[:, c, tl(t)], in_=v32[:, c, tl(t)], func=AF.Copy),
                       reads=[v32_t[c][t]], writes=[vbf_t[c][t]])
                act.op(lambda e, c=c, t=t: e.activation(out=vsq[:, c, tl(t)], in_=v32[:, c, tl(t)], func=AF.Square),
                       reads=[v32_t[c][t]], writes=[vsq_t[c][t]])
        if stop == "conv2":
            return
        tmp1 = ph.sbuf("ctmp", [128, TW], F32)
        tmp1_t = ph.T()
        tmp2 = ph.sbuf("ctmp2", [128, TW], F32)
        tmp2_t = ph.T()
        tmp = [tmp1, tmp2]
        tmp_t = [tmp1_t, tmp2_t]
        rsd = [sgb[0], sgb[1]]
        rsd_t = [sgb_t[0], sgb_t[1]]
        if _os_env("KMARKS"):
            print("SBUF remaining in conv:", nc.sbuf_bytes_remaining)
        cnr = []
        for t in range(NT):
            m_ps, m_t = nextps()
            e_ps, e_t = nextps()
            for c in range(4):
                mm(m_ps[:], onesC[:], vbf[:, c, tl(t)], c == 0, c == 3, [onesC_t, vbf_t[c][t]], m_t)
            for c in range(4):
                mm(e_ps[:], onesC[:], vsq[:, c, tl(t)], c == 0, c == 3, [onesC_t, vsq_t[c][t]], e_t)
            act.op(lambda e, t=t, m_ps=m_ps: e.activation(out=tmp[t][:], in_=m_ps[:], func=AF.Square), reads=[m_t], writes=[tmp_t[t]])
            dve.op(lambda e, t=t, e_ps=e_ps: e.tensor_tensor(out=tmp[t][:], in0=e_ps[:], in1=tmp[t][:], op=ALU.subtract), reads=[e_t, tmp_t[t]], writes=[tmp_t[t]])
            act.op(lambda e, t=t: e.activation(out=tmp[t][:], in_=tmp[t][:], func=AF.Ln, bias=epsln[:], scale=1.0), reads=[tmp_t[t], epsln_t], writes=[tmp_t[t]])
            act.op(lambda e, t=t: e.activation(out=rsd[t][:], in_=tmp[t][:], func=AF.Exp, scale=-0.5), reads=[tmp_t[t]], writes=[rsd_t[t]])
            nm_ps, nm_t = nextps()
            r_ps, r_t = nextps()
            dve.op(lambda e, t=t, nm_ps=nm_ps, m_ps=m_ps: e.scalar_tensor_tensor(out=nm_ps[:], in0=m_ps[:], scalar=-1.0, in1=rsd[t][:], op0=ALU.mult, op1=ALU.mult),
                   reads=[m_t, rsd_t[t]], writes=[nm_t])
            act.op(lambda e, t=t, r_ps=r_ps: e.activation(out=r_ps[:], in_=rsd[t][:], func=AF.Copy), reads=[rsd_t[t]], writes=[r_t])
            cnr.append((nm_ps, nm_t, r_ps, r_t))
        for t in range(NT):
            nm_ps, nm_t, r_ps, r_t = cnr[t]
            for c in range(4):
                dve.op(lambda e, c=c, t=t, r_ps=r_ps: e.tensor_tensor(out=v32[:, c, tl(t)], in0=v32[:, c, tl(t)], in1=r_ps[:], op=ALU.mult),
                       reads=[v32_t[c][t], r_t], writes=[v32_t[c][t]])
                dve.op(lambda e, c=c, t=t, nm_ps=nm_ps: e.tensor_tensor(out=v32[:, c, tl(t)], in0=v32[:, c, tl(t)], in1=nm_ps[:], op=ALU.add),
                       reads=[v32_t[c][t], nm_t], writes=[v32_t[c][t]])
                act.op(lambda e, c=c, t=t: e.activation(out=B.hc[:, c, tl(t)], in_=v32[:, c, tl(t)], func=AF.Silu,
                                                        bias=cvec[:, l, 8 + c:9 + c], scale=cvec[:, l, 4 + c:5 + c]),
                       reads=[v32_t[c][t], cvec_t], writes=[B.hc_t[c][t]])
        ph.close()

    def retention(l, B):
        ph = Phase(K)
        qT = ph.sbuf("qT", [128, TOK], BF16)
        kT = ph.sbuf("kT", [128, TOK], BF16)
        qxf = ph.sbuf("qxf", [128, TOK], BF16)
        qxb = ph.sbuf("qxb", [128, TOK], BF16)
        qk_t = [ph.T() for t in range(NT)]
        kzf = ph.sbuf("kzf", [128, 8, 128], BF16)
        kzb = ph.sbuf("kzb", [128, 8, 128], BF16)
        kz_t = [ph.T() for g in range(2)]
        vtok = ph.sbuf("vtok", [128, 8, 256], BF16)
        vtok_t = [ph.T() for n in range(8)]
        srg = ph.sbuf("srg", [128, 2, TOK], BF16)
        srg_t = [[ph.T() for t in range(NT)] for cc in range(2)]
        Sbf = ph.sbuf("Sbf", [128, 2, 8, 256], BF16)
        Sbf_t = [[ph.T() for n in range(8)] for d in range(2)]
        Sseg = ph.sbuf("Sseg", [128, 2, 4, 256], F32)
        Sseg_t = [[ph.T() for s_ in range(4)] for d in range(2)]
        Srun = ph.sbuf("Srun", [128, 2, 256], F32)
        Srun_t = [ph.T() for d in range(2)]
        Sinit = ph.sbuf("Sinit", [128, 2, 256], F32)
        Sinit_t = [ph.T() for d in range(2)]
        PTb = ph.sbuf("PTb", [128, 8, 128], BF16)
        PTb_t = [ph.T() for g in range(2)]
        ontok = ph.sbuf("ontok", [128, 8, 256], BF16)
        ontok_t = [ph.T() for n in range(8)]
        if _os_env("KMARKS"):
            print("SBUF remaining in retention:", nc.sbuf_bytes_remaining)
        stt = ph.sbuf("stt", [128, 8, 16], F32)
        stt_t = [ph.T() for n in range(8)]
        W = I["mix_w_in"][l]
        HV = {}

        def partA(h):
            mark(f"ret{l}_h{h}")
            slot = wslot()
            _, vq, p1 = wload(W, 1024 + h * 128, 128, 8, slot=slot, off=0)
            _, vk, p2 = wload(W, 1536 + h * 128, 128, 8, slot=slot, off=1024)
            _, vv, p3 = wload(W, 2048 + h * 256, 256, 8, slot=slot, off=2048)
            _, vg, p4 = wload(W, 3072 + h * 256, 256, 8, slot=slot, off=4096)
            pool.dma([p1, p2, p3, p4], slot[2], writes=[slot[1]])
            for d in range(2):
                sp.dma((Sinit[:, d, :], I["s0"][l, d, h]), sinit_d[d], writes=[Sinit_t[d]])
            for t in range(NT):
                q_ps, q_t = nextps()
                k_ps, k_t = nextps()
                for kc in range(8):
                    mm(q_ps[:], vq[:, kc, :], u[:, kc, tl(t)], kc == 0, kc == 7, [slot[1], u_t[kc][t]], q_t)
                for kc in range(8):
                    mm(k_ps[:], vk[:, kc, :], u[:, kc, tl(t)], kc == 0, kc == 7, [slot[1], u_t[kc][t]], k_t)
                dve.op(lambda e, t=t, q_ps=q_ps: e.tensor_copy(out=qT[:, tl(t)], in_=q_ps[:]), reads=[q_t], writes=[qk_t[t]])
                act.op(lambda e, t=t, k_ps=k_ps: e.activation(out=kT[:, tl(t)], in_=k_ps[:], func=AF.Copy), reads=[k_t], writes=[qk_t[t]])
                dve.op(lambda e, t=t, q_ps=q_ps, h=h: e.tensor_tensor(out=qxf[:, tl(t)].rearrange("p (a b) -> p a b", b=128),
                                                                      in0=q_ps[:].rearrange("p (a b) -> p a b", b=128), in1=bc_mid(XIf[:, h, :], 4), op=ALU.mult),
                       reads=[q_t, rtab_t], writes=[qk_t[t]])
                dve.op(lambda e, t=t, q_ps=q_ps, h=h: e.tensor_tensor(out=qxb[:, tl(t)].rearrange("p (a b) -> p a b", b=128),
                                                                      in0=q_ps[:].rearrange("p (a b) -> p a b", b=128), in1=bc_mid(XIb[:, h, :], 4), op=ALU.mult),
                       reads=[q_t, rtab_t], writes=[qk_t[t]])
            for n2 in range(4):
                v_ps, v_t = nextps()
                for sub in range(2):
                    n = n2 * 2 + sub
                    for kc in range(8):
                        mm(v_ps[:, sub * 256:(sub + 1) * 256], u[:, kc, n * 128:(n + 1) * 128], vv[:, kc, :], kc == 0, kc == 7, [slot[1], u_t[kc][n // 4]], v_t)
                act.op(lambda e, n2=n2, v_ps=v_ps: e.activation(out=vtok[:, 2 * n2:2 * n2 + 2, :], in_=v_ps[:].rearrange("p (a b) -> p a b", b=256), func=AF.Copy),
                       reads=[v_t], writes=[vtok_t[2 * n2], vtok_t[2 * n2 + 1]])
            for g in range(2):
                t_ps, t_t = nextps()
                for sub in range(4):
                    n = g * 4 + sub
                    mm(t_ps[:, sub * 128:(sub + 1) * 128], kT[:, n * 128:(n + 1) * 128], identb[:], True, True, [qk_t[g], identb_t], t_t)
                act.op(lambda e, g=g, t_ps=t_ps, h=h: e.activation(out=kzf[:, 4 * g:4 * g + 4, :], in_=t_ps[:].rearrange("p (a b) -> p a b", b=128), func=AF.Identity,
                                                                   scale=rsm[:, 8 + h:9 + h]),
                       reads=[t_t, rtab_t], writes=[kz_t[g]])
                act.op(lambda e, g=g, t_ps=t_ps, h=h: e.activation(out=kzb[:, 4 * g:4 * g + 4, :], in_=t_ps[:].rearrange("p (a b) -> p a b", b=128), func=AF.Identity,
                                                                   scale=rsm[:, 12 + h:13 + h]),
                       reads=[t_t, rtab_t], writes=[kz_t[g]])
            HV[h] = (slot, vg)

        def partB(h):
            slot, vg = HV[h]
            mark(f"ret{l}_h{h}_state")
            orders = [list(range(8)), list(range(7, -1, -1))]
            prevs = [(Sinit[:, d, :], Sinit_t[d]) for d in range(2)]
            for step in range(8):
                for d in range(2):
                    n = orders[d][step]
                    kz = kzf if d == 0 else kzb
                    prev, prev_t = prevs[d]
                    cflag = (n in (2, 4, 6)) if d == 0 else (n in (5, 3, 1))
                    if cflag:
                        act.op(lambda e, d=d, n=n, prev=prev: e.activation(out=Sbf[:, d, n, :], in_=prev, func=AF.Identity, scale=flags[:, 0:1]),
                               reads=[prev_t, flags_t], writes=[Sbf_t[d][n]])
                    else:
                        act.op(lambda e, d=d, n=n, prev=prev: e.activation(out=Sbf[:, d, n, :], in_=prev, func=AF.Copy),
                               reads=[prev_t], writes=[Sbf_t[d][n]])
                    kv_ps, kv_t = nextps()
                    mm(kv_ps[:, 0:256], kz[:, n, :], vtok[:, n, :], True, True, [kz_t[n // 4], vtok_t[n]], kv_t)
                    to_seg = (n % 2 == 1) if d == 0 else (n % 2 == 0)
                    if to_seg:
                        dst, dst_t = Sseg[:, d, n // 2, :], Sseg_t[d][n // 2]
                    else:
                        dst, dst_t = Srun[:, d, :], Srun_t[d]
                    gcol = (24 if cflag else 16) + d * 4 + h
                    dve.op(lambda e, dst=dst, prev=prev, kv_ps=kv_ps, gcol=gcol: e.scalar_tensor_tensor(out=dst, in0=prev, scalar=rsm[:, gcol:gcol + 1], in1=kv_ps[:, 0:256],
                                                                                                        op0=ALU.mult, op1=ALU.add),
                           reads=[prev_t, kv_t, rtab_t], writes=[dst_t])
                    if to_seg:
                        oname = "o_sf" if d == 0 else "o_sb"
                        out_toks.append(sp.dma((O[oname][l, n // 2, h], Sseg[:, d, n // 2, :]), sseg_d[d][n // 2], reads=[dst_t]))
                    prevs[d] = (dst, dst_t)
            mark(f"ret{l}_h{h}_o")
            for g in range(2):
                s_ps, s_t = nextps()
                for sub in range(4):
                    n = g * 4 + sub
                    mm(s_ps[:, sub * 128:(sub + 1) * 128], kT[:, n * 128:(n + 1) * 128], qT[:, n * 128:(n + 1) * 128], True, True, [qk_t[g]], s_t)
                dve.op(lambda e, g=g, s_ps=s_ps, h=h: e.tensor_tensor(out=PTb[:, 4 * g:4 * g + 4, :], in0=s_ps[:].rearrange("p (a b) -> p a b", b=128),
                                                                      in1=bc_mid(Dcomb[:, h, :], 4), op=ALU.mult),
                       reads=[s_t, rtab_t], writes=[PTb_t[g]])
            for cc in range(2):
                for t in range(NT):
                    g_ps, g_t = nextps()
                    for kc in range(8):
                        mm(g_ps[:], vg[:, kc, cc * 128:(cc + 1) * 128], u[:, kc, tl(t)], kc == 0, kc == 7, [slot[1], u_t[kc][t]], g_t)
                    act.op(lambda e, cc=cc, t=t, g_ps=g_ps: e.activation(out=srg[:, cc, tl(t)], in_=g_ps[:], func=AF.Silu), reads=[g_t], writes=[srg_t[cc][t]])

        def partC(h):
            obanks = [nextps() for _ in range(4)]
            for n in range(8):
                o_ps, o_t = obanks[n // 2]
                oc = slice((n % 2) * 256, (n % 2) * 256 + 256)
                mm(o_ps[:, oc], PTb[:, n, :], vtok[:, n, :], True, False, [PTb_t[n // 4], vtok_t[n]], o_t)
                mm(o_ps[:, oc], qxf[:, n * 128:(n + 1) * 128], Sbf[:, 0, n, :], False, False, [qk_t[n // 4], Sbf_t[0][n]], o_t)
                mm(o_ps[:, oc], qxb[:, n * 128:(n + 1) * 128], Sbf[:, 1, n, :], False, True, [qk_t[n // 4], Sbf_t[1][n]], o_t)
            for n in range(8):
                o_ps, o_t = obanks[n // 2]
                oc = slice((n % 2) * 256, (n % 2) * 256 + 256)
                dve.op(lambda e, n=n, o_ps=o_ps, oc=oc: e.bn_stats(out=stt[:, n, 0:6], in_=o_ps[:, oc]), reads=[o_t], writes=[stt_t[n]])
                dve.op(lambda e, n=n: e.bn_aggr(out=stt[:, n, 8:10], in_=stt[:, n, 0:6]), reads=[stt_t[n]], writes=[stt_t[n]])
            act.op(lambda e: e.activation(out=stt[:, :, 10], in_=stt[:, :, 9], func=AF.Sqrt, bias=epsln[:], scale=1.0), reads=stt_t + [epsln_t], writes=stt_t)
            dve.op(lambda e: e.reciprocal(out=stt[:, :, 11], in_=stt[:, :, 10]), reads=stt_t, writes=stt_t)
            for n in range(8):
                o_ps, o_t = obanks[n // 2]
                oc = slice((n % 2) * 256, (n % 2) * 256 + 256)
                dve.op(lambda e, n=n, o_ps=o_ps, oc=oc: e.tensor_scalar(out=ontok[:, n, :], in0=o_ps[:, oc], scalar1=stt[:, n, 8:9], scalar2=stt[:, n, 11:12],
                                                                        op0=ALU.subtract, op1=ALU.mult),
                       reads=[o_t, stt_t[n]], writes=[ontok_t[n]])

        def partD(h):
            for cc in range(2):
                for t in range(NT):
                    r_ps, r_t = nextps()
                    for sub in range(4):
                        n = t * 4 + sub
                        mm(r_ps[:, sub * 128:(sub + 1) * 128], ontok[:, n, cc * 128:(cc + 1) * 128], identb[:], True, True, [ontok_t[n], identb_t], r_t)
                    dve.op(lambda e, cc=cc, t=t, r_ps=r_ps, h=h: e.tensor_tensor(out=B.ogT[:, h * 2 + cc, tl(t)], in0=r_ps[:], in1=srg[:, cc, tl(t)], op=ALU.mult),
                           reads=[r_t, srg_t[cc][t]], writes=[B.ogT_t[h * 2 + cc][t]])


        partA(0)
        for h in range(4):
            partB(h)
            partC(h)
            if h + 1 < 4:
                partA(h + 1)
            partD(h)
        ph.close()

    def mla(l, B):
        W = I["mix_w_in"][l]
        SCALE = 96.0 ** -0.5
        mark(f"mlaC1_{l}")
        outer = Phase(K)
        mqn = outer.sbuf("mqn", [128, 4, TOK], BF16)
        mqn_t = [[outer.T() for t in range(NT)] for c in range(4)]
        ckvall = outer.sbuf("ckvall", [128, 2, 1280], BF16)
        ckvall_t = [[outer.T() for kt in range(3)] for c in range(2)]
        krall = outer.sbuf("krall", [96, 1280], BF16)
        krall_t = [outer.T() for kt in range(3)]
        ph = Phase(K)
        x32 = [ph.sbuf("mx32", [128, 4, TW], F32) for _ in range(1)]
        x32_t = [[ph.T() for c in range(4)] for _ in range(1)]
        sq = [ph.sbuf("msq", [128, 4, TW], BF16) for _ in range(1)]
        sq_t = [[ph.T() for c in range(4)] for _ in range(1)]
        rs = [ph.sbuf("mrs", [128, TW], F32) for _ in range(1)]
        rs_t = [ph.T() for _ in range(1)]
        ckv32 = ph.sbuf("ckv32", [128, 2, TOK], F32)
        ckv32_t = [[ph.T() for t in range(NT)] for c in range(2)]
        kr32 = ph.sbuf("kr32", [96, TOK], F32)
        kr32_t = [ph.T() for t in range(NT)]
        rt1 = ph.sbuf("rt1", [96, TW], F32)
        rt2 = ph.sbuf("rt2", [96, TW], F32)
        rt_t = ph.T()
        cst32 = ph.sbuf("cst32", [128, 2, 256], F32)
        cst32_t = ph.T()
        krc32 = ph.sbuf("krc32", [96, 256], F32)
        krc32_t = ph.T()
        if _os_env("KMARKS"):
            print("SBUF remaining in mla C1:", nc.sbuf_bytes_remaining)
        sp.dma((cst32[:], I["ckvT"][l].rearrange("(c p) k -> p c k", p=128)), cch_d, writes=[cst32_t])
        sp.dma((krc32[:], I["krT96"][l]), cch_d, writes=[krc32_t])
        seal_consts([cst32_t, krc32_t], cch_d)
        for c in range(2):
            act.op(lambda e, c=c: e.activation(out=ckvall[:, c, 0:256], in_=cst32[:, c, :], func=AF.Copy), reads=[cst32_t], writes=[ckvall_t[c][0]])
        act.op(lambda e: e.activation(out=krall[64:96, 0:256], in_=krc32[64:96, :], func=AF.Copy), reads=[krc32_t], writes=[krall_t[0]])
        slotA, vA, pA = wload(W, 4096, 512, 8)
        pool.dma([pA], slotA[2], writes=[slotA[1]])
        slot = wslot()
        _, v1, p1 = wload(W, 4608, 288, 8, slot=slot, off=0)
        _, vsw, p2 = wload(I["w_kr96_sw"][l], 0, 96, 8, slot=slot, off=2304)
        pool.dma([p1, p2], slot[2], writes=[slot[1]])
        k = 0
        for t in range(NT):
            for (nch, col0, ones_ap, ones_tt, ncol0, is_q) in ((4, 0, onesC, onesC_t, 0, True), (2, 0, onesK, onesK_t, 4, False)):
                b = 0
                wv, wsl = (vA, slotA) if is_q else (v1, slot)
                for c in range(nch):
                    x_ps, x_t = nextps()
                    for kc in range(8):
                        mm(x_ps[:], wv[:, kc, col0 + c * 128:col0 + (c + 1) * 128], u[:, kc, tl(t)], kc == 0, kc == 7, [wsl[1], u_t[kc][t]], x_t)
                    dve.op(lambda e, b=b, c=c, x_ps=x_ps: e.tensor_copy(out=x32[b][:, c, :], in_=x_ps[:]), reads=[x_t], writes=[x32_t[b][c]])
                    act.op(lambda e, b=b, c=c: e.activation(out=sq[b][:, c, :], in_=x32[b][:, c, :], func=AF.Square), reads=[x32_t[b][c]], writes=[sq_t[b][c]])
                m_ps, m_t = nextps()
                for c in range(nch):
                    mm(m_ps[:], ones_ap[:], sq[b][:, c, :], c == 0, c == nch - 1, [ones_tt, sq_t[b][c]], m_t)
                act.op(lambda e, b=b, m_ps=m_ps: e.activation(out=rs[b][:], in_=m_ps[:], func=AF.Ln, bias=epsrms[:], scale=1.0), reads=[m_t, epsrms_t], writes=[rs_t[b]])
                act.op(lambda e, b=b: e.activation(out=rs[b][:], in_=rs[b][:], func=AF.Exp, scale=-0.5), reads=[rs_t[b]], writes=[rs_t[b]])
                for c in range(nch):
                    gcol = mlan[:, l, ncol0 + c:ncol0 + c + 1]
                    if is_q:
                        dve.op(lambda e, b=b, c=c, t=t, gcol=gcol: e.scalar_tensor_tensor(out=mqn[:, c, tl(t)], in0=x32[b][:, c, :], scalar=gcol, in1=rs[b][:],
                                                                                          op0=ALU.mult, op1=ALU.mult),
                               reads=[x32_t[b][c], rs_t[b], mlan_t], writes=[mqn_t[c][t]])
                    else:
                        dve.op(lambda e, b=b, c=c, t=t, gcol=gcol: e.scalar_tensor_tensor(out=ckv32[:, c, tl(t)], in0=x32[b][:, c, :], scalar=gcol, in1=rs[b][:],
                                                                                          op0=ALU.mult, op1=ALU.mult),
                               reads=[x32_t[b][c], rs_t[b], mlan_t], writes=[ckv32_t[c][t]])
                        act.op(lambda e, c=c, t=t: e.activation(out=ckvall[:, c, 256 + t * TW:256 + (t + 1) * TW], in_=ckv32[:, c, tl(t)], func=AF.Copy),
                               reads=[ckv32_t[c][t]], writes=[ckvall_t[c][1 + t]])
            kr_ps, kr_t = nextps()
            ks_ps, ks_t = nextps()
            for kc in range(8):
                mm(kr_ps[0:96, :], v1[:, kc, 192:288], u[:, kc, tl(t)], kc == 0, kc == 7, [slot[1], u_t[kc][t]], kr_t)
            for kc in range(8):
                mm(ks_ps[0:96, :], vsw[:, kc, :], u[:, kc, tl(t)], kc == 0, kc == 7, [slot[1], u_t[kc][t]], ks_t)
            dve.op(lambda e, t=t, kr_ps=kr_ps: e.tensor_tensor(out=rt1[64:96, :], in0=kr_ps[64:96, :], in1=ropeC[64:96, tl(t)], op=ALU.mult), reads=[kr_t, ropeC_t, rt_t], writes=[rt_t])
            dve.op(lambda e, t=t, ks_ps=ks_ps: e.tensor_tensor(out=rt2[64:96, :], in0=ks_ps[64:96, :], in1=ropeS[64:96, tl(t)], op=ALU.mult), reads=[ks_t, ropeS_t, rt_t], writes=[rt_t])
            dve.op(lambda e, t=t: e.tensor_tensor(out=kr32[64:96, tl(t)], in0=rt1[64:96, :], in1=rt2[64:96, :], op=ALU.add), reads=[rt_t], writes=[kr32_t[t]])
            act.op(lambda e, t=t: e.activation(out=krall[64:96, 256 + t * TW:256 + (t + 1) * TW], in_=kr32[64:96, tl(t)], func=AF.Copy), reads=[kr32_t[t]], writes=[krall_t[1 + t]])
        out_toks.append(sp.dma((O["o_ckvT"][l].rearrange("(c p) t -> p c t", p=128), ckv32[:]), ckvo_d, reads=[ckv32_t[c][t] for c in range(2) for t in range(NT)]))
        out_toks.append(sp.dma((O["o_krT"][l], kr32[64:96, :]), kro_d, reads=kr32_t))
        ph.close()
        mark(f"mlaC2_{l}")
        ph = Phase(K)
        VO = ph.sbuf("VO", [128, 10, 8, 128], BF16)
        VO_t = [ph.T() for kc in range(10)]
        dve.op(lambda e: e.memset(VO[:], 1.0), writes=VO_t)
        ET = [ph.sbuf("ET", [128, TW], BF16) for _ in range(3)]
        ET_t = [ph.T() for _ in range(3)]
        qt1 = ph.sbuf("qt1", [96, TW], F32)
        qt2 = ph.sbuf("qt2", [96, TW], F32)
        qt_t = ph.T()
        rden = [ph.sbuf("rden", [128, TW], F32) for _ in range(2)]
        rdst = [ph.sbuf("rdst", [128, TW], F32) for _ in range(2)]
        rdst_t = [ph.T() for _ in range(2)]
        if _os_env("KMARKS"):
            print("SBUF remaining in mla C2:", nc.sbuf_bytes_remaining)
        rden_t = [ph.T() for _ in range(2)]
        slotq = wslot()
        _, vuq, p1 = wload(I["mla_w_uq"][l], 0, 768, 4, slot=slotq, off=0)
        _, vuqs, p2 = wload(I["mla_w_uq_sw"][l], 0, 768, 4, slot=slotq, off=3072)
        pool.dma([p1, p2], slotq[2], writes=[slotq[1]])
        slotk, vkv, p3 = wload(I["mla_w_ukv"][l], 0, 1024, 2)
        pool.dma([p3], slotk[2], writes=[slotk[1]])
        ring_reserved.update([slot_index(slotq), slot_index(slotk)])
        vkv4 = vkv.rearrange("p a (h two d) -> p a h two d", two=2, d=64)
        for kc in range(10):
            v_ps, v_t = nextps()
            kt = 0 if kc < 2 else (1 if kc < 6 else 2)
            for a in range(2):
                mm(v_ps[:], ckvall[:, a, kc * 128:(kc + 1) * 128], vkv4[:, a, :, 1, :], a == 0, a == 1, [slotk[1], ckvall_t[a][kt]], v_t)
            v4 = v_ps[:].rearrange("p (hp two d) -> p hp two d", two=2, d=64)
            VOk = VO[:, kc, :, :].rearrange("p (hp two) c -> p hp two c", two=2)
            act.op(lambda e, v4=v4, VOk=VOk: e.activation(out=VOk[:, :, 0, 0:64], in_=v4[:, :, 0, :], func=AF.Copy), reads=[v_t], writes=[VO_t[kc]])
            act.op(lambda e, v4=v4, VOk=VOk: e.activation(out=VOk[:, :, 1, 64:128], in_=v4[:, :, 1, :], func=AF.Copy), reads=[v_t], writes=[VO_t[kc]])
        pj_i = [0]
        bank7_free = (not ADA_FINS) and all(v == 2 for v in ada_done.values())
        pj_banks = (2, 3, 7) if bank7_free else (2, 3)
        if _os_env("KMARKS"):
            print("bank7_free", l, bank7_free)

        def projps():
            i = pj_banks[pj_i[0] % len(pj_banks)]
            pj_i[0] += 1
            return PS[i], PS_t[i]

        def proj_pieces(h):
            b = h % 2
            pieces = []
            for kt, (k0, kw) in enumerate(((0, 256), (256, 512), (768, 512))):
                def knope(kt=kt, k0=k0, kw=kw):
                    n_ps, n_t = projps()
                    for a in range(2):
                        mm(n_ps[0:64, 0:kw], vkv[:, a, h * 128:h * 128 + 64], ckvall[:, a, k0:k0 + kw], a == 0, a == 1, [slotk[1], ckvall_t[a][kt]], n_t)
                    act.op(lambda e: e.activation(out=KTa[b][0:64, k0:k0 + kw], in_=n_ps[0:64, 0:kw], func=AF.Copy), reads=[n_t], writes=[KTa_t[b]])
                    if kt == 0:
                        dve.op(lambda e: e.tensor_copy(out=KTa[b][64:96, :], in_=krall[64:96, :]), reads=krall_t, writes=[KTa_t[b]])
                pieces.append(knope)
            for t in range(NT):
                def qproj(t=t):
                    q_ps, q_t = projps()
                    qs_ps, qs_t = projps()
                    for a in range(4):
                        mm(q_ps[0:96, :], vuq[:, a, h * 96:(h + 1) * 96], mqn[:, a, tl(t)], a == 0, a == 3, [slotq[1], mqn_t[a][t]], q_t)
                    for a in range(4):
                        mm(qs_ps[0:96, :], vuqs[:, a, h * 96:(h + 1) * 96], mqn[:, a, tl(t)], a == 0, a == 3, [slotq[1], mqn_t[a][t]], qs_t)
                    dve.op(lambda e: e.tensor_tensor(out=qt1[:], in0=q_ps[0:96, :], in1=ropeC[:, tl(t)], op=ALU.mult), reads=[q_t, ropeC_t, qt_t], writes=[qt_t])
                    dve.op(lambda e: e.tensor_tensor(out=qt2[:], in0=qs_ps[0:96, :], in1=ropeS[:, tl(t)], op=ALU.mult), reads=[qs_t, ropeS_t, qt_t], writes=[qt_t])
                    dve.op(lambda e: e.tensor_tensor(out=QTa[b][0:96, tl(t)], in0=qt1[:], in1=qt2[:], op=ALU.add), reads=[qt_t], writes=[QTa_t[b]])
                pieces.append(qproj)
            return pieces

        def proj(h):
            for p_ in proj_pieces(h):
                p_()

        nxt_pieces = []

        ekc = [0]

        def attn(h):
            b = h % 2
            mark(f"mla{l}_h{h}")
            po = (h % 2) * 64
            dn = 64 - po
            for t in range(NT):
                it = h * NT + t
                ob = it % 2
                o_ps, o_t = PS[ob], PS_t[ob]
                SK = 2
                rs_ = {}
                for kc in range(10 + SK):
                    if kc < 10:
                        sb_ = 4 + (ekc[0] % 3)
                        s_ps, s_t = PS[sb_], PS_t[sb_]
                        mm(s_ps[:], KTa[b][:, kc * 128:(kc + 1) * 128], QTa[b][:, tl(t)], True, True, [KTa_t[b], QTa_t[b]], s_t)
                        r = ekc[0] % 3
                        ekc[0] += 1
                        rs_[kc] = r
                        act.op(lambda e, r=r, s_ps=s_ps: e.activation(out=ET[r][:], in_=s_ps[:], func=AF.Exp, scale=SCALE), reads=[s_t], writes=[ET_t[r]])
                        if nxt_pieces and kc in ((2, 5, 8) if t == 0 else (2, 6)):
                            nxt_pieces.pop(0)()
                    pk = kc - SK
                    if pk >= 0:
                        pr = rs_[pk]
                        mm(o_ps[:], VO[:, pk, h, :], ET[pr][:], pk == 0, pk == 9, [VO_t[pk], ET_t[pr]], o_t)
                rb = it % 2
                if fin_q:
                    finish_one()
                dve.op(lambda e, rb=rb, o_ps=o_ps, dn=dn: e.reciprocal(out=rden[rb][dn:dn + 64, :], in_=o_ps[dn:dn + 64, :]), reads=[o_t], writes=[rden_t[rb]])
                sp.dma((rdst[rb][po:po + 64, :], rden[rb][dn:dn + 64, :]), rd_d[rb], reads=[rden_t[rb]], writes=[rdst_t[rb]])
                fin_q.append((rb, o_ps, o_t, po, h, t))

        fin_q = []

        def finish_one():
            rb, o_ps, o_t, po, h, t = fin_q.pop(0)
            dve.op(lambda e: e.tensor_tensor(out=B.attnT[po:po + 64, h // 2, tl(t)], in0=o_ps[po:po + 64, :], in1=rdst[rb][po:po + 64, :], op=ALU.mult),
                   reads=[o_t, rdst_t[rb]], writes=[B.attnT_t[h // 2][t]])

        proj(0)
        for h in range(8):
            if h + 1 < 8:
                nxt_pieces.extend(proj_pieces(h + 1))
            attn(h)
            while nxt_pieces:
                nxt_pieces.pop(0)()
        while fin_q:
            finish_one()
        ring_reserved.clear()
        ph.close()
        outer.close()

    def merge(l, B, post=None):
        W = I["mix_w_in"][l]
        mark(f"merge{l}")
        ph = Phase(K)
        merged = ph.sbuf("merged", [128, 8, TOK], BF16)
        merged_t = [[ph.T() for t in range(NT)] for c in range(8)]
        ph2 = Phase(K)
        sg = [[ph2.sbuf("msg", [128, TW], F32) for b in range(3)] for _ in range(2)]
        sg_t = [[ph2.T() for b in range(3)] for _ in range(2)]
        mt = [[ph2.sbuf("mmt", [128, TW], F32) for b in range(3)] for _ in range(2)]
        mt_t = [[ph2.T() for b in range(3)] for _ in range(2)]
        if _os_env("KMARKS"):
            print("SBUF remaining in merge:", nc.sbuf_bytes_remaining)
        k = 0
        for c in range(8):
            slot = wslot()
            _, vc, p1 = wload(I["conv_w_out"][l], c * 128, 128, 4, slot=slot, off=0)
            _, vr, p2 = wload(I["ret_w_out"][l], c * 128, 128, 8, slot=slot, off=512)
            _, vm, p3 = wload(I["mla_w_out"][l], c * 128, 128, 4, slot=slot, off=1536)
            vg = []
            pg = []
            for b in range(3):
                _, v_, p_ = wload(W, 4896 + b * 1024 + c * 128, 128, 8, slot=slot, off=2048 + b * 1024)
                vg.append(v_)
                pg.append(p_)
            pool.dma([p1, p2, p3] + pg, slot[2], writes=[slot[1]])
            for t in range(NT):
                kb = k % 2
                k += 1
                br = []
                for (vw, nk, src, src_t) in ((vc, 4, B.hc, B.hc_t), (vr, 8, B.ogT, B.ogT_t), (vm, 4, B.attnT, B.attnT_t)):
                    b_ps, b_t = nextps()
                    for a in range(nk):
                        mm(b_ps[:], vw[:, a, :], src[:, a, tl(t)], a == 0, a == nk - 1, [slot[1], src_t[a][t]], b_t)
                    br.append((b_ps, b_t))
                for b in range(3):
                    g_ps, g_t = nextps()
                    for a in range(8):
                        mm(g_ps[:], vg[b][:, a, :], u[:, a, tl(t)], a == 0, a == 7, [slot[1], u_t[a][t]], g_t)
                    act.op(lambda e, kb=kb, b=b, g_ps=g_ps: e.activation(out=sg[kb][b][:], in_=g_ps[:], func=AF.Sigmoid), reads=[g_t], writes=[sg_t[kb][b]])
                for b in range(3):
                    b_ps, b_t = br[b]
                    dve.op(lambda e, kb=kb, b=b, b_ps=b_ps: e.tensor_tensor(out=mt[kb][b][:], in0=b_ps[:], in1=sg[kb][b][:], op=ALU.mult),
                           reads=[b_t, sg_t[kb][b]], writes=[mt_t[kb][b]])
                dve.op(lambda e, kb=kb: e.tensor_tensor(out=mt[kb][0][:], in0=mt[kb][0][:], in1=mt[kb][1][:], op=ALU.add),
                       reads=[mt_t[kb][0], mt_t[kb][1]], writes=[mt_t[kb][0]])
                dve.op(lambda e, kb=kb, c=c, t=t: e.tensor_tensor(out=merged[:, c, tl(t)], in0=mt[kb][0][:], in1=mt[kb][2][:], op=ALU.add),
                       reads=[mt_t[kb][0], mt_t[kb][2]], writes=[merged_t[c][t]])
        ph2.close()
        mark(f"mixo{l}")
        for c in range(8):
            if c % 4 == 0:
                slot, view, pair = wload(I["mix_w_o"][l], (c // 4) * 512, 512, 8)
                pool.dma([pair], slot[2], writes=[slot[1]])
            cc = c % 4
            for t in range(NT):
                o_ps, o_t = nextps()
                for a in range(8):
                    mm(o_ps[:], view[:, a, cc * 128:(cc + 1) * 128], merged[:, a, tl(t)], a == 0, a == 7, [slot[1], merged_t[a][t]], o_t)
                dve.op(lambda e, o_ps=o_ps, c=c, t=t: e.scalar_tensor_tensor(out=xs[:, c, tl(t)], in0=o_ps[:], scalar=tabG[:, l, 8 + c:8 + c + 1],
                                                                             in1=xs[:, c, tl(t)], op0=ALU.mult, op1=ALU.add),
                       reads=[o_t, tab_t[l][1], xs_t[c][t]], writes=[xs_t[c][t]])
        if post is not None:
            post()
        ph.close()

    for j in range(3):
        ada(0, j)
    for c in range(8):
        for t in range(NT):
            act.op(lambda e, c=c, t=t: e.activation(out=u[:, c, tl(t)], in_=xs[:, c, tl(t)], func=AF.Identity,
                                                    bias=tabQ[:, 0, c:c + 1], scale=tabP[:, 0, c:c + 1]),
                   reads=[xs_t[c][t], tab_t[0][0]], writes=[u_t[c][t]])
            dve.op(lambda e, c=c, t=t: e.tensor_scalar(out=xs[:, c, tl(t)], in0=xs[:, c, tl(t)], scalar1=ALPHA, scalar2=None, op0=ALU.mult),
                   reads=[xs_t[c][t]], writes=[xs_t[c][t]])

    def dump(buf, buf_t, nch):
        for c in range(nch):
            for t in range(NT):
                act.op(lambda e, c=c, t=t: e.activation(out=xs[:, c, tl(t)], in_=buf[:, c, tl(t)], func=AF.Copy), reads=[buf_t[c][t]], writes=[xs_t[c][t]])

    for l in range(DEPTH):
        last = (l == DEPTH - 1)
        ffn(l, I["ffn1_w_in"][l], I["ffn1_w_out"][l], 0, hooks=ret_table_parts(l), nhost_tail=(1 if l == 0 else 0),
            post=lambda: layer_norm(l, 0))
        if stop in ("ln1", "rt"):
            break
        mx = Phase(K)
        B = mixer_bufs(mx)
        conv_branch(l, B)
        if stop in ("conv1", "conv2"):
            break
        if stop == "conv":
            dump(B.hc, B.hc_t, 4)
            break
        retention(l, B)
        if stop == "ret":
            dump(B.ogT, B.ogT_t, 8)
            break
        mla(l, B)
        if stop == "mla":
            dump(B.attnT, B.attnT_t, 4)
            break
        merge(l, B, post=lambda: layer_norm(l, 1))
        mx.close()
        if stop == "mix":
            break
        ffn(l, I["ffn2_w_in"][l], I["ffn2_w_out"][l], 2,
            post=lambda: layer_norm(l, 2, final=last))
        if stop == "l0":
            break

    out_d = K.dsem("out")
    yT = O["yT"].rearrange("(c p) t -> p c t", p=128)
    toks = []
    for c in range(8):
        toks.append(sp.dma((yT[:, c, :], xs[:, c, :]), out_d, reads=[xs_t[c][0], xs_t[c][1]]))
    mark("end")
    K.finish(toks + out_toks)
    import os as _os
    if _os.environ.get("KMARKS"):
        import json as _json
        _json.dump(MARKS, open(_os.environ["KMARKS"], "w"))
    st_holder.append(st)
    return nc


st_holder = []

INPUT_SHAPES = {
    "xT": (D, TOK),
    "cond": (128, 8),
    "ada_w": (DEPTH, D, 9 * D),
    "ada_bL": (DEPTH, 128, 72),
    "lngL": (DEPTH, 128, 24),
    "lnbL": (DEPTH, 128, 24),
    "ffn1_w_in": (DEPTH, D, 2 * DFF),
    "ffn1_w_out": (DEPTH, DFF, D),
    "ffn2_w_in": (DEPTH, D, 2 * DFF),
    "ffn2_w_out": (DEPTH, DFF, D),
    "mix_w_in": (DEPTH, D, MIXW),
    "w_kr96_sw": (DEPTH, D, 96),
    "conv_wdwL": (DEPTH, 128, 4, 31),
    "conv_vecL": (DEPTH, 128, 12),
    "conv_w_out": (DEPTH, 512, D),
    "ret_decayL": (DEPTH, 128, 8),
    "ret_w_out": (DEPTH, D, D),
    "mla_normL": (DEPTH, 128, 6),
    "mla_w_uq": (DEPTH, 512, 768),
    "mla_w_uq_sw": (DEPTH, 512, 768),
    "mla_w_ukv": (DEPTH, 256, 1024),
    "mla_w_out": (DEPTH, 512, D),
    "mix_w_o": (DEPTH, D, D),
    "ident": (128, 128),
    "retc": (6, 128, 128),
    "colz": (128, 2),
    "flags": (128, 4),
    "ropeC": (96, TOK),
    "ropeS": (96, TOK),
    "qmask": (32, TOK),
    "kmask": (32, 1280),
    "ckvT": (DEPTH, 256, 256),
    "krT96": (DEPTH, 96, 256),
    "s0": (DEPTH, 2, 4, 128, 256),
}
OUTPUT_SHAPES = {
    "yT": (D, TOK),
    "o_ckvT": (DEPTH, 256, TOK),
    "o_krT": (DEPTH, 32, TOK),
    "o_sf": (DEPTH, 4, 4, 128, 256),
    "o_sb": (DEPTH, 4, 4, 128, 256),
}


def _chunkT(v, n):
    return np.ascontiguousarray(v.reshape(n, 128).T)


def _rope_tables(sample):
    C = np.ones((96, TOK), np.float32)
    S = np.zeros((96, TOK), np.float32)
    if sample:
        t = np.arange(TOK)
        row = (t // 64).astype(np.float32)
        col = (t % 64).astype(np.float32)
        inv = (np.float32(10000.0) ** (-np.arange(8, dtype=np.float32) / np.float32(8.0))).astype(np.float32)
        ang = np.stack([row[:, None] * inv[None, :], col[:, None] * inv[None, :]], axis=1).astype(np.float32)
        cs = np.cos(ang).astype(np.float32)
        sn = np.sin(ang).astype(np.float32)
        for axis in range(2):
            for half in range(2):
                r0 = 64 + axis * 16 + half * 8
                C[r0:r0 + 8, :] = cs[:, axis, :].T
                S[r0:r0 + 8, :] = (-sn[:, axis, :].T if half == 0 else sn[:, axis, :].T)
    return C, S


def prepare_inputs(inp):
    f = lambda a: np.ascontiguousarray(np.asarray(a, dtype=np.float32))
    g = lambda k: np.asarray(inp[k], dtype=np.float32)
    mix = g("mix_w_in")
    perm32 = np.arange(32) ^ 8
    perm96 = np.concatenate([np.arange(64), 64 + perm32])
    uq = g("mla_w_uq")
    uq_sw = uq.reshape(DEPTH, 512, 8, 96)[:, :, :, perm96].reshape(DEPTH, 512, 768)
    kr_sw = np.concatenate([mix[:, :, 4800:4864], mix[:, :, 4864 + perm32]], axis=2)
    j = np.arange(128)[:, None].astype(np.float32)
    i = np.arange(128)[None, :].astype(np.float32)
    ks = np.float32(128.0 ** -0.5)
    retc = np.stack([np.maximum(i - j, 0), np.maximum(j - i, 0), (i >= j) * ks, (j > i) * ks,
                     np.broadcast_to(i + 1, (128, 128)), np.broadcast_to(128 - i, (128, 128))]).astype(np.float32)
    colz = np.stack([127 - np.arange(128), np.arange(128)], axis=1).astype(np.float32)
    dec = np.concatenate([g("ret_decay_fwd"), g("ret_decay_bwd")], axis=1)
    conv_vec = np.stack([np.concatenate([_chunkT(g("conv_b_dw")[l], 4), _chunkT(g("conv_ln_g")[l], 4), _chunkT(g("conv_ln_b")[l], 4)], axis=1)
                         for l in range(DEPTH)])
    mla_norm = np.stack([np.concatenate([_chunkT(g("mla_q_norm")[l], 4), _chunkT(g("mla_kv_norm")[l], 2)], axis=1) for l in range(DEPTH)])
    wdw = g("conv_w_dw")
    wdwL = wdw.reshape(DEPTH, 31, 4, 128).transpose(0, 3, 2, 1)
    tseg = np.arange(TOK) // 256
    qmask = np.zeros((32, TOK), np.float32)
    qmask[0] = 1.0
    for s_ in range(4):
        qmask[1 + s_] = (tseg == s_)
    BIG = 30000.0
    kb = np.arange(1280) // 256
    kmask_p = np.zeros((32, 1280), np.float32)
    kmask_p[0] = -BIG
    for s_ in range(4):
        kmask_p[1 + s_] = BIG * (kb == s_ + 1)
    shared = {
        "ada_w": f(inp["ada_w"]),
        "ada_bL": f(np.stack([_chunkT(g("ada_b")[l], 72) for l in range(DEPTH)])),
        "lngL": f(np.stack([_chunkT(g("post_ln_g")[l].reshape(-1), 24) for l in range(DEPTH)])),
        "lnbL": f(np.stack([_chunkT(g("post_ln_b")[l].reshape(-1), 24) for l in range(DEPTH)])),
        "ffn1_w_in": f(inp["ffn1_w_in"]), "ffn1_w_out": f(inp["ffn1_w_out"]),
        "ffn2_w_in": f(inp["ffn2_w_in"]), "ffn2_w_out": f(inp["ffn2_w_out"]),
        "mix_w_in": f(mix), "w_kr96_sw": f(kr_sw),
        "conv_wdwL": f(wdwL), "conv_vecL": f(conv_vec), "conv_w_out": f(inp["conv_w_out"]),
        "ret_decayL": f(np.broadcast_to(dec[:, None, :], (DEPTH, 128, 8))), "ret_w_out": f(inp["ret_w_out"]),
        "mla_normL": f(mla_norm), "mla_w_uq": f(uq), "mla_w_uq_sw": f(uq_sw),
        "mla_w_ukv": f(inp["mla_w_ukv"]), "mla_w_out": f(inp["mla_w_out"]), "mix_w_o": f(inp["mix_w_o"]),
        "ident": f(np.eye(128)), "retc": f(retc), "colz": f(colz), "qmask": f(qmask),
    }
    ropes = {False: _rope_tables(False), True: _rope_tables(True)}
    maps = []
    for core in range(8):
        m = dict(shared)
        sample = core >= 4
        if not sample:
            x = np.asarray(inp["x_prompt"][4 * core:4 * core + 4]).reshape(TOK, D)
            cv = np.asarray(inp["c_ctx"])
            m["ckvT"] = np.zeros((DEPTH, 256, 256), np.float32)
            m["krT96"] = np.zeros((DEPTH, 96, 256), np.float32)
            m["s0"] = np.zeros((DEPTH, 2, 4, 128, 256), np.float32)
            m["kmask"] = f(kmask_p)
            m["flags"] = np.zeros((128, 4), np.float32)
        else:
            b = core - 4
            x = np.asarray(inp["x_sample"][b])
            cv = np.asarray(inp["c"][b])
            m["ckvT"] = f(np.transpose(g("cache_mla_ckv")[b], (0, 2, 1)))
            kr = np.zeros((DEPTH, 96, 256), np.float32)
            kr[:, 64:96, :] = np.transpose(g("cache_mla_krope")[b], (0, 2, 1))
            m["krT96"] = kr
            m["s0"] = f(np.stack([g("state_ret_fwd")[b], g("state_ret_bwd")[b]], axis=1))
            m["kmask"] = np.zeros((32, 1280), np.float32)
            fl = np.zeros((128, 4), np.float32)
            fl[:, 0] = 1.0
            m["flags"] = fl
        m["ropeC"], m["ropeS"] = ropes[sample]
        m["xT"] = f(x.T)
        m["cond"] = f(_chunkT(cv, 8))
        maps.append(m)
    return maps


_NC_CACHE = {}


def run(inp, stop=None):
    if stop not in _NC_CACHE:
        _NC_CACHE[stop] = build_program(stop)
    nc = _NC_CACHE[stop]
    maps = prepare_inputs(inp)
    maps = [{k: np.ascontiguousarray(v, dtype=np.float32) for k, v in m.items() if k in INPUT_SHAPES} for m in maps]
    res = run_bass_kernel_spmd(nc, maps, core_ids=list(range(8)))
    return res.results


def kernel(**inputs):
    res = run(inputs)
    y_prompt = np.zeros((16, 256, D), np.float32)
    y_sample = np.zeros((4, 1024, D), np.float32)
    n_ckv = np.zeros((16, DEPTH, 256, 256), np.float32)
    n_kr = np.zeros((16, DEPTH, 256, 32), np.float32)
    n_sf = np.zeros((16, DEPTH, 4, 128, 256), np.float32)
    n_sb = np.zeros((16, DEPTH, 4, 128, 256), np.float32)
    for core in range(8):
        r = res[core]
        y = np.asarray(r["yT"]).T
        if core < 4:
            y_prompt[4 * core:4 * core + 4] = y.reshape(4, 256, D)
            ck = np.asarray(r["o_ckvT"])
            kr = np.asarray(r["o_krT"])
            sf = np.asarray(r["o_sf"])
            sb = np.asarray(r["o_sb"])
            for s_ in range(4):
                q = 4 * core + s_
                n_ckv[q] = np.transpose(ck[:, :, s_ * 256:(s_ + 1) * 256], (0, 2, 1))
                n_kr[q] = np.transpose(kr[:, :, s_ * 256:(s_ + 1) * 256], (0, 2, 1))
                n_sf[q] = sf[:, s_]
                n_sb[q] = sb[:, s_]
        else:
            y_sample[core - 4] = y
    return (y_prompt, y_sample, n_ckv, n_kr, n_sf, n_sb)
```
